# Optimizing a Trainium2 kernel written in Bass

```python
import jax, jax.numpy as jnp
from jax import lax
import numpy as np

D_MODEL = 1024
BATCH = 16
SEQ = 2048
DEPTH = 1

CHUNK = 64
GLA_HEADS = 4
GLA_DK = D_MODEL // 2 // GLA_HEADS
GLA_DV = D_MODEL // GLA_HEADS
GLA_K_WIDTH = GLA_HEADS * GLA_DK
GLA_V_WIDTH = GLA_HEADS * GLA_DV
GK_RANK = 16
GATE_NORMALIZER = 16.0
ATT_HEADS = 16
ATT_DH = 64
ATT_WIDTH = ATT_HEADS * ATT_DH
LEFT_CHUNKS = 8
PAD = LEFT_CHUNKS * CHUNK
BAND = PAD + CHUNK
MAX_REL = 256
N_BRANCH = 2
RMS_EPS = 1e-6

IN_SPLITS = [GLA_K_WIDTH, GLA_K_WIDTH, GLA_V_WIDTH, GLA_V_WIDTH, GK_RANK,
             ATT_WIDTH, ATT_WIDTH, ATT_WIDTH, ATT_WIDTH, N_BRANCH * D_MODEL]
IN_COLS = int(sum(IN_SPLITS))

kernel_name = "hybrid_gla_chunkattn_gated_block"


def rms_norm(x, g):
    xf = x.astype(jnp.float32)
    y = xf * lax.rsqrt(jnp.mean(xf * xf, axis=-1, keepdims=True) + RMS_EPS)
    return (y * g.astype(jnp.float32)).astype(x.dtype)


def gla_branch(q, k, v, gk_code, g_path, gk_up, gk_bias, gla_norm_g, w_o_gla):
    B, T, _ = q.shape
    NC = T // CHUNK
    q = q.reshape(B, NC, CHUNK, GLA_HEADS, GLA_DK) * (GLA_DK ** -0.5)
    k = k.reshape(B, NC, CHUNK, GLA_HEADS, GLA_DK)
    v = v.reshape(B, NC, CHUNK, GLA_HEADS, GLA_DV)
    gk = jax.nn.log_sigmoid((gk_code @ gk_up + gk_bias).astype(jnp.float32)) / GATE_NORMALIZER
    gk = gk.reshape(B, NC, CHUNK, GLA_HEADS, GLA_DK)
    a_cum = jnp.cumsum(gk, axis=2)
    a_end = a_cum[:, :, -1]
    k_dec = (k.astype(jnp.float32) * jnp.exp(a_end[:, :, None] - a_cum)).astype(k.dtype)
    upd = jnp.einsum('bnchk,bnchv->bnhkv', k_dec, v)
    decay = jnp.exp(a_end).astype(upd.dtype)

    def step(state, inp):
        d, u = inp
        state = d[..., None] * state + u
        return state, state

    init = jnp.zeros((B, GLA_HEADS, GLA_DK, GLA_DV), upd.dtype)
    _, s_all = lax.scan(step, init, (jnp.moveaxis(decay, 1, 0), jnp.moveaxis(upd, 1, 0)))
    s_all = jnp.moveaxis(s_all, 0, 1)
    o = jnp.einsum('bnchk,bnhkv->bnchv', q, s_all)
    o = rms_norm(o, gla_norm_g).reshape(B, T, GLA_V_WIDTH)
    return (o * jax.nn.silu(g_path)) @ w_o_gla


def chunk_attention_branch(q, k, v, g_path, rel_bias, w_o_att):
    B, T, _ = q.shape
    NC = T // CHUNK
    q = q.reshape(B, T, ATT_HEADS, ATT_DH)
    k = k.reshape(B, T, ATT_HEADS, ATT_DH)
    v = v.reshape(B, T, ATT_HEADS, ATT_DH)
    k_pad = jnp.pad(k, ((0, 0), (PAD, 0), (0, 0), (0, 0)))
    v_pad = jnp.pad(v, ((0, 0), (PAD, 0), (0, 0), (0, 0)))
    dist = np.arange(CHUNK)[:, None] + PAD - np.arange(BAND)[None, :]
    rel_idx = np.clip(dist, -MAX_REL, MAX_REL) + MAX_REL
    bias = rel_bias[:, rel_idx].astype(jnp.float32)
    band_pos = jnp.arange(BAND)
    scale = ATT_DH ** -0.5

    def attend(c):
        qb = lax.dynamic_slice_in_dim(q, c * CHUNK, CHUNK, axis=1)
        kb = lax.dynamic_slice_in_dim(k_pad, c * CHUNK, BAND, axis=1)
        vb = lax.dynamic_slice_in_dim(v_pad, c * CHUNK, BAND, axis=1)
        s = jnp.einsum('bqhd,bkhd->bhqk', qb, kb).astype(jnp.float32) * scale + bias
        valid = band_pos >= PAD - c * CHUNK
        s = jnp.where(valid, s, -1e30)
        p = jax.nn.softmax(s, axis=-1).astype(vb.dtype)
        return jnp.einsum('bhqk,bkhd->bqhd', p, vb)

    o = lax.map(attend, jnp.arange(NC))
    o = jnp.transpose(o, (1, 0, 2, 3, 4)).reshape(B, T, ATT_WIDTH)
    return (o * jax.nn.silu(g_path)) @ w_o_att


def setup_inputs(seed: int = 0) -> dict:
    key = jax.random.key(seed)
    ks = jax.random.split(key, 12)
    f32 = jnp.float32
    return {
        "x": jax.random.normal(ks[0], (BATCH, SEQ, D_MODEL), f32),
        "norm_pre_g": 1.0 + 0.05 * jax.random.normal(ks[1], (D_MODEL,), f32),
        "w_in": jax.random.normal(ks[2], (D_MODEL, IN_COLS), f32) * D_MODEL ** -0.5,
        "gk_up": jax.random.normal(ks[3], (GK_RANK, GLA_K_WIDTH), f32) * GK_RANK ** -0.5,
        "gk_bias": 0.02 * jax.random.normal(ks[4], (GLA_K_WIDTH,), f32),
        "gla_norm_g": 1.0 + 0.05 * jax.random.normal(ks[5], (GLA_DV,), f32),
        "rel_bias": 0.1 * jax.random.normal(ks[6], (ATT_HEADS, 2 * MAX_REL + 1), f32),
        "w_o_gla": jax.random.normal(ks[7], (GLA_V_WIDTH, D_MODEL), f32) * GLA_V_WIDTH ** -0.5,
        "w_o_att": jax.random.normal(ks[8], (ATT_WIDTH, D_MODEL), f32) * ATT_WIDTH ** -0.5,
        "merge_bias": 0.02 * jax.random.normal(ks[9], (N_BRANCH * D_MODEL,), f32),
        "w_out": jax.random.normal(ks[10], (D_MODEL, D_MODEL), f32) * D_MODEL ** -0.5,
        "norm_post_g": 1.0 + 0.05 * jax.random.normal(ks[11], (D_MODEL,), f32),
    }


def reference(x, norm_pre_g, w_in, gk_up, gk_bias, gla_norm_g, rel_bias,
              w_o_gla, w_o_att, merge_bias, w_out, norm_post_g):
    splits = [int(s) for s in np.cumsum(IN_SPLITS)[:-1]]
    for _layer in range(DEPTH):
        h = rms_norm(x, norm_pre_g)
        proj = h @ w_in
        (q_g, k_g, v_g, g_g, gk_code,
         q_a, k_a, v_a, g_a, gate_logits) = jnp.split(proj, splits, axis=-1)
        y_gla = gla_branch(q_g, k_g, v_g, gk_code, g_g, gk_up, gk_bias, gla_norm_g, w_o_gla)
        y_att = chunk_attention_branch(q_a, k_a, v_a, g_a, rel_bias, w_o_att)
        gates = jax.nn.sigmoid(gate_logits + merge_bias)
        gate_gla, gate_att = jnp.split(gates, 2, axis=-1)
        merged = gate_gla * y_gla + gate_att * y_att
        y = merged @ w_out
        x = x + rms_norm(y, norm_post_g)
    return x
```

```python
import os
import numpy as np
import ml_dtypes
import concourse.bass as bass
import concourse.mybir as mybir
from concourse.bass_utils import run_bass_kernel_spmd

F32 = mybir.dt.float32
BF16 = mybir.dt.bfloat16
U8 = mybir.dt.uint8
AF = mybir.ActivationFunctionType
ALU = mybir.AluOpType
AX = mybir.AxisListType

N_CORES = 8
D = 1024
T = 2048
NSEQ = 2
KC = 8
NTT = T // 128
NTB = T // 512
EPS = 1e-6
NGROUPS = 24
G_B0, G_Q, G_K, G_V0, G_V1, G_G0, G_G1, G_D0, G_O0 = 0, 8, 9, 10, 11, 12, 13, 14, 22
NEG = -30000.0


class Op:
    __slots__ = ("eng", "fn", "deps", "idx", "signal", "count", "sem", "is_dma", "waits", "done")

    def __init__(self, eng, fn, is_dma, sem):
        self.eng = eng
        self.fn = fn
        self.deps = set()
        self.signal = False
        self.count = 0
        self.sem = sem
        self.is_dma = is_dma
        self.waits = []
        self.done = None


class Sched:
    ENGS = ("pe", "act", "dve", "pool", "sp")

    def __init__(self):
        self.ops = []
        self.per_eng = {e: [] for e in self.ENGS}
        self.last_w = {}
        self.readers = {}
        self.dma_sems = []
        self.last_by_sem = {}
        self.bar_deps = []
        self.bar_seen = set()

    def barrier(self):
        self.bar_deps = [o for s, o in self.last_by_sem.items() if not (isinstance(s, str) and s.startswith("w"))]
        self.bar_seen = set()

    def add(self, eng, fn, r=(), w=(), dma=None):
        is_dma = dma is not None
        op = Op(eng, fn, is_dma, dma if is_dma else eng)
        if is_dma and dma not in self.dma_sems:
            self.dma_sems.append(dma)
        op.idx = len(self.ops)
        deps = op.deps
        if self.bar_deps and eng != "pe" and eng not in self.bar_seen:
            self.bar_seen.add(eng)
            for o in self.bar_deps:
                deps.add(o)
        self.last_by_sem[op.sem] = op
        for k in r:
            lw = self.last_w.get(k)
            if lw is not None:
                deps.add(lw)
        for k in w:
            lw = self.last_w.get(k)
            if lw is not None and (lw.eng != eng or lw.is_dma or is_dma or eng != 'pe'):
                deps.add(lw)
            rd = self.readers.get(k)
            if rd:
                for o in rd.values():
                    if o.eng != eng or o.is_dma or is_dma or eng != 'pe':
                        deps.add(o)
        for k in r:
            self.readers.setdefault(k, {})[(eng, op.idx) if is_dma else eng] = op
        for k in w:
            self.last_w[k] = op
            self.readers[k] = {}
        self.ops.append(op)
        self.per_eng[eng].append(op)
        return op

    def finalize(self):
        for op in self.ops:
            for d in op.deps:
                d.signal = True
        cnt = {}
        for op in self.ops:
            if op.is_dma:
                cnt[op.sem] = cnt.get(op.sem, 0) + 16
                op.count = cnt[op.sem]
            elif op.signal:
                cnt[op.sem] = cnt.get(op.sem, 0) + 1
                op.count = cnt[op.sem]
        self.final_counts = cnt
        clocks = {e: {} for e in self.ENGS}
        for op in self.ops:
            clk = clocks[op.eng]
            waits = {}
            for d in sorted(op.deps, key=lambda o: o.idx):
                if clk.get(d.sem, 0) < d.count:
                    waits[d.sem] = max(waits.get(d.sem, 0), d.count)
                    for s, v in d.done.items():
                        if clk.get(s, 0) < v:
                            clk[s] = v
            op.waits = list(waits.items())
            done = dict(clk)
            if op.is_dma or op.signal:
                if done.get(op.sem, 0) < op.count:
                    done[op.sem] = op.count
            op.done = done

    def emit(self, name, eng, sems):
        for op in self.per_eng[name]:
            for s, v in op.waits:
                eng.wait_ge(sems[s], v)
            ins = op.fn(eng)
            if op.is_dma:
                ins.then_inc(sems[op.sem], 16)
            elif op.signal:
                ins.then_inc(sems[op.sem], 1)


def build_nc(debug=None, nseq=NSEQ, phases="ABCD"):
    debug = debug or {}
    nc = bass.Bass("TRN2", target_bir_lowering=False)
    S = Sched()

    x_d = nc.dram_tensor("x", [NSEQ, T, D], F32, kind="ExternalInput")
    wg_d = nc.dram_tensor("wg", [NGROUPS, 128, KC * 512], F32, kind="ExternalInput")
    wcode_d = nc.dram_tensor("wcode", [128, KC * 16], F32, kind="ExternalInput")
    gkup_d = nc.dram_tensor("gkup", [17, 512], F32, kind="ExternalInput")
    vecs_d = nc.dram_tensor("vecs", [3, 1024], F32, kind="ExternalInput")
    mbias_d = nc.dram_tensor("mbias", [128, 16], F32, kind="ExternalInput")
    relb_d = nc.dram_tensor("relb", [16, 128, 640], F32, kind="ExternalInput")
    ident_d = nc.dram_tensor("ident", [128, 128], F32, kind="ExternalInput")
    revtri_d = nc.dram_tensor("revtri", [128, 128], F32, kind="ExternalInput")
    cind_d = nc.dram_tensor("cind", [128, 2], F32, kind="ExternalInput")
    out_d = nc.dram_tensor("out", [NSEQ, T, D], F32, kind="ExternalOutput")
    dbg_d = {}
    for name, shape in debug.items():
        dbg_d[name] = nc.dram_tensor("dbg_" + name, list(shape), F32, kind="ExternalOutput")

    hT = nc.alloc_sbuf_tensor("hT", [128, KC, T], BF16)
    zaT = nc.alloc_sbuf_tensor("zaT", [128, KC, T], BF16)
    zgT = nc.alloc_sbuf_tensor("zgT", [128, KC, T], BF16)
    NWB = 5
    wb = [nc.alloc_sbuf_tensor("wb%d" % i, [128, KC, 512], BF16) for i in range(NWB)]
    ident = nc.alloc_sbuf_tensor("ident_bf", [128, 128], BF16)
    identf = nc.alloc_sbuf_tensor("ident_f", [128, 128], F32)
    revtri = nc.alloc_sbuf_tensor("revtri_sb", [128, 128], F32)
    cind = nc.alloc_sbuf_tensor("cind_sb", [128, 2], F32)
    mbias = nc.alloc_sbuf_tensor("mbias_sb", [128, 16], F32)
    gkup = nc.alloc_sbuf_tensor("gkup_sb", [17, 512], F32)
    wcode = nc.alloc_sbuf_tensor("wcode_sb", [128, KC, 16], BF16)
    neghalf = nc.alloc_sbuf_tensor("neghalf", [128, 8], F32)
    revtri_b = nc.alloc_sbuf_tensor("revtri_b", [128, 128], BF16)
    cind_b = nc.alloc_sbuf_tensor("cind_b", [128, 2], BF16)
    gkup_b = nc.alloc_sbuf_tensor("gkup_b", [17, 512], BF16)
    stat = nc.alloc_sbuf_tensor("stat", [128, 64], F32)
    ARENA_BYTES = 68288
    arena = nc.alloc_sbuf_tensor("arena", [128, ARENA_BYTES], U8)
    ps = nc.alloc_psum_tensor("ps", [128, 4096], F32)
    ps_bf = ps.bitcast(BF16)

    class Arena:
        def __init__(self):
            self.off = 0

        def take(self, free_shape, dtype):
            n = int(np.prod(free_shape)) * mybir.dt.size(dtype)
            n_al = (n + 31) // 32 * 32
            assert self.off + n_al <= ARENA_BYTES, ("arena overflow", self.off, n_al)
            ap = arena[:, self.off:self.off + n].bitcast(dtype)
            self.off += n_al
            if len(free_shape) == 2:
                ap = ap.rearrange("p (a b) -> p a b", a=free_shape[0])
            elif len(free_shape) == 3:
                ap = ap.rearrange("p (a b c) -> p a b c", a=free_shape[0], b=free_shape[1])
            elif len(free_shape) == 4:
                ap = ap.rearrange("p (a b c d) -> p a b c d", a=free_shape[0], b=free_shape[1], c=free_shape[2])
            return ap

    def bank(b, n=512, off=0):
        return ps[:, b * 512 + off: b * 512 + off + n]

    def bank_bf(b, n=1024, off=0):
        return ps_bf[:, b * 1024 + off: b * 1024 + off + n]

    def cload(dst_ap, src_ap, key, tmp=None):
        S.add("sp", lambda e, d=dst_ap, s=src_ap: e.dma_start(out=d, in_=s), w=[key], dma="c_" + str(key))

    cload(identf[:], ident_d.ap(), "identf")
    cload(revtri[:], revtri_d.ap(), "revtri")
    cload(cind[:], cind_d.ap(), "cind")
    cload(mbias[:], mbias_d.ap(), "mbias")
    cload(gkup[:], gkup_d.ap(), "gkup")
    S.add("dve", lambda e: e.tensor_copy(out=ident[:], in_=identf[:]), r=["identf"], w=["ident"])
    S.add("pool", lambda e: e.memset(neghalf[:], -0.5), w=["neghalf"])
    S.add("dve", lambda e: e.tensor_copy(out=revtri_b[:], in_=revtri[:]), r=["revtri"], w=["revtri_b"])
    S.add("dve", lambda e: e.tensor_copy(out=cind_b[:], in_=cind[:]), r=["cind"], w=["cind_b"])
    S.add("dve", lambda e: e.tensor_copy(out=gkup_b[:], in_=gkup[:]), r=["gkup"], w=["gkup_b"])
    S.add("dve", lambda e: e.tensor_scalar(out=mbias[:], in0=mbias[:], scalar1=0.5, scalar2=None, op0=ALU.mult),
          r=["mbias"], w=["mbias"])
    S.add("pool", lambda e: e.dma_start(out=wcode[:].rearrange("p a b -> p (a b)"), in_=wcode_d.ap()),
          w=["wcode"], dma="c_wcode")

    per_seq = []
    if "B" in phases:
        per_seq += [G_B0 + i for i in range(8)]
    if "C" in phases:
        per_seq += [G_Q, G_K, G_V0, G_V1, G_G0, G_G1]
    if "D" in phases:
        per_seq += [G_D0 + i for i in range(8)] + [G_O0, G_O0 + 1]
    wsched = per_seq * nseq
    wstate = {"issued": 0, "pos": 0}
    released = set()

    def w_try_issue(limit=None):
        while wstate["issued"] < (len(wsched) if limit is None else min(limit, len(wsched))) and (wstate["issued"] < NWB or (wstate["issued"] - NWB) in released):
            j = wstate["issued"]
            i = j % NWB
            g = wsched[j]
            S.add("pool", lambda e, i=i, g=g: e.dma_start(out=wb[i][:].rearrange("p a b -> p (a b)"), in_=wg_d[g]),
                  w=[("wb", i)], dma="w%d" % i)
            wstate["issued"] += 1

    def wnext(expect_g):
        pos = wstate["pos"]
        assert wsched[pos] == expect_g, (pos, wsched[pos], expect_g)
        w_try_issue()
        assert wstate["issued"] > pos, ("weight slot not released in time", pos)
        wstate["pos"] += 1
        return pos % NWB, pos

    def wrelease(pos):
        released.add(pos)
        w_try_issue()

    stat_n = {"n": 0}

    def stat_col(n=1):
        c = stat_n["n"] % (64 // 4) * 4
        stat_n["n"] += 1
        return c

    def do_seq(seq):
        S.barrier()
        A = Arena()
        gpre = A.take([1024], F32)
        NXT = 8
        xt = [A.take([1024], F32) for _ in range(NXT)]
        sqr = [A.take([1024], BF16) for _ in range(2)]
        NHB = 4
        hb = [A.take([1024], BF16) for _ in range(NHB)]
        S.add("sp", lambda e, d=gpre: e.dma_start(out=d, in_=vecs_d[0:1, :].partition_broadcast(128)),
              w=["gpre"], dma="c_gpre")
        def a_load(tt, seq=seq):
            xi = xt[tt % NXT]
            kx = ("xt", tt % NXT)
            S.add("sp", lambda e: e.dma_start(out=xi, in_=x_d[seq, tt * 128:(tt + 1) * 128, :]),
                  w=[kx], dma="xt%d" % (tt % NXT))
            c = stat_col()
            ssq = stat[:, c:c + 1]
            ks = ("stat", c)
            S.add("act", lambda e, sq=sqr[tt % 2]: e.activation(
                out=sq, in_=xi, func=AF.Square, accum_out=ssq), r=[kx], w=[ks, ("sq", tt % 2)])
            S.add("dve", lambda e: e.tensor_scalar(out=ssq, in0=ssq, scalar1=1.0 / D, scalar2=EPS,
                                                   op0=ALU.mult, op1=ALU.add), r=[ks], w=[ks])
            S.add("pool", lambda e: e.tensor_tensor(out=ssq, in0=ssq, in1=neghalf[:, 0:1], op=ALU.pow),
                  r=[ks, "neghalf"], w=[ks])
            return ssq, ks

        def a_norm(tt, ssq, ks):
            xi = xt[tt % NXT]
            hi = hb[tt % NHB]
            kx = ("xt", tt % NXT)
            kh = ("hb", tt % NHB)
            S.add("dve", lambda e: e.scalar_tensor_tensor(
                out=hi, in0=xi, scalar=ssq, in1=gpre, op0=ALU.mult, op1=ALU.mult),
                r=[kx, ks, "gpre"], w=[kh])
            pb = 2 * (tt % 4)
            for kc in range(KC):
                S.add("pe", lambda e, kc=kc: e.matmul(
                    bank(pb + kc // 4, 128, (kc % 4) * 128), lhsT=hi[:, kc * 128:(kc + 1) * 128], rhs=ident[:],
                    start=True, stop=True),
                    r=[kh, "ident"], w=[("ps", pb + kc // 4)])

        def a_copy(tt):
            pb = 2 * (tt % 4)
            kp = [("ps", pb), ("ps", pb + 1)]
            src = ps[:, pb * 512:(pb + 2) * 512].rearrange("p (a b) -> p a b", a=KC)
            dst = hT[:, :, tt * 128:(tt + 1) * 128]
            if tt % 2 == 0:
                S.add("act", lambda e: e.copy(out=dst, in_=src), r=kp, w=[("hT", tt)] + kp)
            else:
                S.add("dve", lambda e: e.tensor_copy(out=dst, in_=src), r=kp, w=[("hT", tt)] + kp)

        a_stats = {}
        for it in range(NTT + 2):
            if it < NTT:
                a_stats[it] = a_load(it)
            if 0 <= it - 1 < NTT:
                a_norm(it - 1, *a_stats[it - 1])
            if 0 <= it - 2 < NTT:
                a_copy(it - 2)

        if "hT" in dbg_d and seq == 0:
            S.barrier()
            D2 = Arena()
            D2.off = 32 * 1024
            tmp = D2.take([2048], F32)
            for kc in range(KC):
                S.add("dve", lambda e, kc=kc: e.tensor_copy(out=tmp, in_=hT[:, kc, :]),
                      r=[("hT", t) for t in range(NTT)], w=["dbgtmp"])
                S.add("sp", lambda e, kc=kc: e.dma_start(out=dbg_d["hT"][:, kc * T:(kc + 1) * T], in_=tmp),
                      r=["dbgtmp"], dma="dbg")


        if "B" in phases:
            S.barrier()
            B = Arena()
            qTb, kTb, Vaugb, gsAb = [], [], [], []
            for _p in range(2):
                qTb.append(B.take([T], BF16))
                kTb.append(B.take([T], BF16))
                Vaugb.append(B.take([NTT, 2, 65], BF16))
                gsAb.append(B.take([NTT, 128], BF16))
            tgA = [B.take([2, 128], F32) for _ in range(1)]
            eB = B.take([2, 640], F32)
            eBb = B.take([2, 640], BF16)
            sbp = [B.take([2, 640], BF16) for _ in range(2)]
            PT = [B.take([2, 640], BF16) for _ in range(6)]
            zall = B.take([NTT, 128], BF16)
            for p in range(2):
                S.add("pool", lambda e, p=p: e.memset(Vaugb[p][:, :, :, 64:65], 2.0), w=[("vaug_ones", p)])
            proj_ring = {"n": 0}
            smain_ring = {"n": 0}
            hT_all = [("hT", t) for t in range(NTT)]

            def proj_items(hp):
                p = hp % 2
                wi, wpos = wnext(G_B0 + hp)
                W = wb[wi]
                kw = ("wb", wi)
                qT, kT, Vaug, gsA = qTb[p], kTb[p], Vaugb[p], gsAb[p]
                items = []

                def fm(which, tb):
                    dstT = qT if which == 0 else kT
                    kd = ("qT", p, tb) if which == 0 else ("kT", p, tb)
                    pb = proj_ring["n"] % 2
                    proj_ring["n"] += 1
                    for kc in range(KC):
                        S.add("pe", lambda e, pb=pb, kc=kc: e.matmul(
                            bank(pb), lhsT=W[:, kc, which * 128:(which + 1) * 128],
                            rhs=hT[:, kc, tb * 512:(tb + 1) * 512], start=(kc == 0), stop=(kc == KC - 1)),
                            r=[kw] + hT_all[tb * 4:tb * 4 + 4], w=[("ps", pb)])
                    if which == 0:
                        S.add("act", lambda e, pb=pb: e.activation(
                            out=dstT[:, tb * 512:(tb + 1) * 512], in_=bank(pb), func=AF.Copy, scale=0.125),
                            r=[("ps", pb)], w=[kd, ("ps", pb)])
                    else:
                        S.add("dve", lambda e, pb=pb: e.tensor_copy(
                            out=dstT[:, tb * 512:(tb + 1) * 512], in_=bank(pb)),
                            r=[("ps", pb)], w=[kd, ("ps", pb)])

                def tm(tp):
                    pb = proj_ring["n"] % 2
                    proj_ring["n"] += 1
                    for j in range(2):
                        tt = 2 * tp + j
                        for kc in range(KC):
                            S.add("pe", lambda e, pb=pb, kc=kc, tt=tt, j=j: e.matmul(
                                bank(pb, 256, j * 256), lhsT=hT[:, kc, tt * 128:(tt + 1) * 128],
                                rhs=W[:, kc, 256:512], start=(kc == 0), stop=(kc == KC - 1)),
                                r=[kw, ("hT", tt)], w=[("ps", pb)])
                    pview = bank(pb).rearrange("p (a b) -> p a b", a=2)
                    vsrc = pview[:, :, 0:128].rearrange("p a (h d) -> p a h d", h=2)
                    S.add("dve", lambda e: e.tensor_copy(out=Vaug[:, 2 * tp:2 * tp + 2, :, 0:64], in_=vsrc),
                          r=[("ps", pb)], w=[("V", p, tp), ("ps", pb)])
                    tg = tgA[0]
                    S.add("act", lambda e: e.activation(out=tg, in_=pview[:, :, 128:256], func=AF.Tanh, scale=0.5),
                          r=[("ps", pb)], w=[("tgA", 0), ("ps", pb)])
                    S.add("dve", lambda e: e.scalar_tensor_tensor(
                        out=gsA[:, 2 * tp:2 * tp + 2, :], in0=tg, scalar=1.0, in1=pview[:, :, 128:256],
                        op0=ALU.add, op1=ALU.mult),
                        r=[("ps", pb), ("tgA", 0)], w=[("gsA", p, tp), ("ps", pb)])

                for tb in range(NTB):
                    items.append(lambda tb=tb: fm(0, tb))
                    items.append(lambda tb=tb: fm(1, tb))
                for tp in range(NTT // 2):
                    items.append(lambda tp=tp: tm(tp))
                items.append(lambda: wrelease(wpos))
                return items

            def qg_items():
                qgT_ = Arena().take([4, T], BF16)
                alias = ([("qT", 0, t) for t in range(NTB)] + [("kT", 0, t) for t in range(NTB)]
                         + [("V", 0, t) for t in range(NTT // 2)] + [("gsA", 0, t) for t in range(NTT // 2)]
                         + [("vaug_ones", 0)])
                wi, wpos_q = wnext(G_Q)
                W = wb[wi]
                kw = ("wb", wi)
                items = []

                def one(cb, tb, n_ev):
                    pb = proj_ring["n"] % 2
                    proj_ring["n"] += 1
                    for kc in range(KC):
                        S.add("pe", lambda e, kc=kc: e.matmul(
                            bank(pb), lhsT=W[:, kc, cb * 128:(cb + 1) * 128],
                            rhs=hT[:, kc, tb * 512:(tb + 1) * 512], start=(kc == 0), stop=(kc == KC - 1)),
                            r=[kw] + hT_all[tb * 4:tb * 4 + 4], w=[("ps", pb)])
                    dst = qgT_[:, cb, tb * 512:(tb + 1) * 512]
                    if n_ev % 2 == 0:
                        S.add("act", lambda e: e.activation(out=dst, in_=bank(pb), func=AF.Copy, scale=128 ** -0.5),
                              r=[("ps", pb)], w=[("qgT", tb), ("ps", pb)] + alias)
                    else:
                        S.add("dve", lambda e: e.tensor_scalar(
                            out=dst, in0=bank(pb), scalar1=128 ** -0.5, scalar2=None, op0=ALU.mult),
                            r=[("ps", pb)], w=[("qgT", tb), ("ps", pb)] + alias)

                n_ev = 0
                for cb in range(4):
                    for tb in range(NTB):
                        items.append(lambda cb=cb, tb=tb, n_ev=n_ev: one(cb, tb, n_ev))
                        n_ev += 1
                items.append(lambda: wrelease(wpos_q))
                return items

            pending = proj_items(0)
            for hp in range(8):
                p = hp % 2
                qT, kT, Vaug, gsA = qTb[p], kTb[p], Vaugb[p], gsAb[p]
                for f in pending:
                    f()
                if hp + 1 < 8:
                    pending = proj_items(hp + 1)
                elif "C" in phases:
                    pending = qg_items()
                else:
                    pending = []
                for h in range(2):
                    head = hp * 2 + h
                    S.add("sp", lambda e, h=h, head=head: e.dma_start(out=eB[:, h, :], in_=relb_d[head]),
                          w=[("eBl", h)], dma="eB%d" % h)
                    S.add("pool", lambda e, h=h: e.memset(eB[64:128, h, 0:64], NEG), r=[("eBl", h)], w=[("eBl", h)])
                    S.add("pool", lambda e, h=h: e.memset(eB[0:64, h, 576:640], NEG), r=[("eBl", h)], w=[("eBl", h)])
                S.add("act", lambda e: e.activation(out=eBb, in_=eB, func=AF.Exp),
                      r=[("eBl", 0), ("eBl", 1)], w=["eB"])

                def qk_step(kt, p=p, qT=qT, kT=kT):
                    nq = min(640, T - 128 * kt)
                    nmain = min(512, nq)
                    ntail = nq - nmain
                    sbi = kt % 2
                    sb = sbp[sbi]
                    rk = [("kT", p, kt // 4)] + [("qT", p, t) for t in range(kt // 4, min(NTB, (128 * kt + nq - 1) // 512 + 1))]
                    sms = []
                    for h in range(2):
                        sm = 2 + smain_ring["n"] % 3
                        smain_ring["n"] += 1
                        sms.append(sm)
                        lhsT = kT[64 * h:64 * h + 64, 128 * kt:128 * kt + 128]
                        S.add("pe", lambda e, sm=sm, lhsT=lhsT, h=h: e.matmul(
                            bank(sm, nmain), lhsT=lhsT, rhs=qT[64 * h:64 * h + 64, 128 * kt:128 * kt + nmain],
                            start=True, stop=True), r=rk, w=[("ps", sm)])
                    if ntail:
                        for h in range(2):
                            lhsT = kT[64 * h:64 * h + 64, 128 * kt:128 * kt + 128]
                            S.add("pe", lambda e, lhsT=lhsT, h=h: e.matmul(
                                bank(5 + h, ntail), lhsT=lhsT,
                                rhs=qT[64 * h:64 * h + 64, 128 * kt + 512:128 * kt + 512 + ntail],
                                start=True, stop=True), r=rk, w=[("ps", 5 + h)])
                        for h in range(2):
                            S.add("act", lambda e, h=h: e.activation(
                                out=sb[:, h, 512:512 + ntail], in_=bank(5 + h, ntail), func=AF.Exp),
                                r=[("ps", 5 + h)], w=[("sb", sbi, h), ("ps", 5 + h)])
                    for h in range(2):
                        S.add("act", lambda e, h=h, sm=sms[h]: e.activation(
                            out=sb[:, h, 0:nmain], in_=bank(sm, nmain), func=AF.Exp),
                            r=[("ps", sms[h])], w=[("sb", sbi, h), ("ps", sms[h])])
                    S.add("dve", lambda e: e.tensor_tensor(
                        out=PT[kt % 6][:, :, 0:nq], in0=sb[:, :, 0:nq], in1=eBb[:, :, 0:nq], op=ALU.mult),
                        r=[("sb", sbi, 0), ("sb", sbi, 1), "eB"], w=[("PT", kt % 6, 0), ("PT", kt % 6, 1)])

                def pv_step(qt, p=p, Vaug=Vaug, gsA=gsA):
                    pv = bank(7, 130).rearrange("p (h d) -> p h d", h=2)
                    kts = list(range(max(0, qt - 4), qt + 1))
                    for h in range(2):
                        for i, k2 in enumerate(kts):
                            S.add("pe", lambda e, h=h, k2=k2, i=i: e.matmul(
                                pv[:, h, :], lhsT=PT[k2 % 6][:, h, (qt - k2) * 128:(qt - k2 + 1) * 128],
                                rhs=Vaug[:, k2, h, :], start=(i == 0), stop=(i == len(kts) - 1)),
                                r=[("PT", k2 % 6, h), ("V", p, k2 // 2), ("vaug_ones", p)], w=[("ps", 7)])
                    c = stat_col()
                    rden = stat[:, c:c + 2]
                    S.add("dve", lambda e: e.reciprocal(out=rden, in_=pv[:, :, 64]),
                          r=[("ps", 7)], w=[("stat", c), ("ps", 7)])
                    for h in range(2):
                        S.add("dve", lambda e, h=h: e.scalar_tensor_tensor(
                            out=zall[:, qt, 64 * h:64 * h + 64], in0=pv[:, h, 0:64], scalar=rden[:, h:h + 1],
                            in1=gsA[:, qt, 64 * h:64 * h + 64], op0=ALU.mult, op1=ALU.mult),
                            r=[("ps", 7), ("stat", c), ("gsA", p, qt // 2)], w=[("zall", qt), ("ps", 7)])

                for step in range(NTT + 1):
                    if step < NTT:
                        qk_step(step)
                    if pending:
                        pending.pop(0)()
                    if step >= 1:
                        pv_step(step - 1)
                for qb in range(4):
                    pb = proj_ring["n"] % 2
                    proj_ring["n"] += 1
                    for j in range(4):
                        qt = qb * 4 + j
                        S.add("pe", lambda e, pb=pb, j=j, qt=qt: e.matmul(
                            bank(pb, 128, j * 128), lhsT=zall[:, qt, :], rhs=ident[:], start=True, stop=True),
                            r=[("zall", qt), "ident"], w=[("ps", pb)])
                    if qb % 2 == 0:
                        S.add("act", lambda e, pb=pb, qb=qb, hp=hp: e.copy(
                            out=zaT[:, hp, qb * 512:(qb + 1) * 512], in_=bank(pb)),
                            r=[("ps", pb)], w=[("zaT", qb // 2), ("ps", pb)])
                    else:
                        S.add("dve", lambda e, pb=pb, qb=qb, hp=hp: e.tensor_copy(
                            out=zaT[:, hp, qb * 512:(qb + 1) * 512], in_=bank(pb)),
                            r=[("ps", pb)], w=[("zaT", qb // 2), ("ps", pb)])

            for f in pending:
                f()
            pending = []

            if "zaT" in dbg_d and seq == 0:
                S.barrier()
                D2 = Arena()
                D2.off = 56 * 1024
                tmp = D2.take([2048], F32)
                for kc in range(KC):
                    S.add("dve", lambda e, kc=kc: e.tensor_copy(out=tmp, in_=zaT[:, kc, :]),
                          r=[("zaT", 0), ("zaT", 1)], w=["dbgtmp"])
                    S.add("sp", lambda e, kc=kc: e.dma_start(out=dbg_d["zaT"][:, kc * T:(kc + 1) * T], in_=tmp),
                          r=["dbgtmp"], dma="dbg")


        if "C" in phases:
            S.barrier()
            C = Arena()
            qgT = C.take([4, T], BF16)
            codeT = C.take([T], BF16)
            decT = C.take([NTT, 4, 2], F32)
            Sst = C.take([4, 256], F32)
            Sb = [C.take([4, 256], BF16) for _ in range(2)]
            gnb = C.take([1024], F32)
            lR = [C.take([2, 512], BF16) for _ in range(2)]
            l32 = [C.take([2, 512], F32) for _ in range(2)]
            erevR = [C.take([512], F32) for _ in range(2)]
            kdecR = [C.take([512], BF16) for _ in range(2)]
            vR = [C.take([1024], BF16) for _ in range(2)]
            gsR = [C.take([1024], F32) for _ in range(2)]
            zR = [C.take([1024], BF16) for _ in range(2)]
            rr = {"n": 0}

            def nb(k=1):
                if k == 2 and rr["n"] % 2:
                    rr["n"] += 1
                b_ = rr["n"] % 8
                rr["n"] += k
                return b_

            hT_all = [("hT", t) for t in range(NTT)]
            S.add("sp", lambda e: e.dma_start(out=gnb, in_=vecs_d[2:3, :].partition_broadcast(128)),
                  w=["gnb"], dma="c_gnb")
            S.add("dve", lambda e: e.tensor_scalar(out=gnb, in0=gnb, scalar1=0.5, scalar2=None, op0=ALU.mult),
                  r=["gnb"], w=["gnb"])
            S.add("pool", lambda e: e.memset(codeT[0:32, :], 1.0), w=["codeT"])
            S.add("pool", lambda e: e.memset(Sst, 0.0), w=[("Sst", h) for h in range(4)])
            if "B" not in phases:
                wi, wpos_q = wnext(G_Q)
                W = wb[wi]
                kw = ("wb", wi)
                n_ev = 0
                for cb in range(4):
                    for tb in range(NTB):
                        pb = nb()
                        for kc in range(KC):
                            S.add("pe", lambda e, pb=pb, kc=kc, tb=tb, cb=cb, W=W: e.matmul(
                                bank(pb), lhsT=W[:, kc, cb * 128:(cb + 1) * 128],
                                rhs=hT[:, kc, tb * 512:(tb + 1) * 512], start=(kc == 0), stop=(kc == KC - 1)),
                                r=[kw] + hT_all[tb * 4:tb * 4 + 4], w=[("ps", pb)])
                        dst = qgT[:, cb, tb * 512:(tb + 1) * 512]
                        if n_ev % 2 == 0:
                            S.add("act", lambda e, pb=pb, dst=dst: e.activation(
                                out=dst, in_=bank(pb), func=AF.Copy, scale=128 ** -0.5),
                                r=[("ps", pb)], w=[("qgT", tb), ("ps", pb)])
                        else:
                            S.add("dve", lambda e, pb=pb, dst=dst: e.tensor_scalar(
                                out=dst, in0=bank(pb), scalar1=128 ** -0.5, scalar2=None, op0=ALU.mult),
                                r=[("ps", pb)], w=[("qgT", tb), ("ps", pb)])
                        n_ev += 1
                wrelease(wpos_q)
            for tb in range(NTB):
                pb = nb()
                for kc in range(KC):
                    S.add("pe", lambda e, pb=pb, kc=kc, tb=tb: e.matmul(
                        ps[0:16, pb * 512:(pb + 1) * 512], lhsT=wcode[:, kc, :],
                        rhs=hT[:, kc, tb * 512:(tb + 1) * 512], start=(kc == 0), stop=(kc == KC - 1)),
                        r=["wcode"] + hT_all[tb * 4:tb * 4 + 4], w=[("ps", pb)])
                S.add("dve", lambda e, pb=pb, tb=tb: e.tensor_copy(
                    out=codeT[0:16, tb * 512:(tb + 1) * 512], in_=ps[0:16, pb * 512:(pb + 1) * 512]),
                    r=[("ps", pb), "codeT"], w=[("codeT", tb), ("ps", pb)])
            wk_i, wpos_k = wnext(G_K)
            wv_i = [wnext(G_V0)[0], wnext(G_V1)[0]]
            wgg_i = [wnext(G_G0)[0], wnext(G_G1)[0]]
            Wk = wb[wk_i]
            Wv = [wb[wv_i[0]], wb[wv_i[1]]]
            Wg = [wb[wgg_i[0]], wb[wgg_i[1]]]
            rr4 = {"n": 0}

            def nb4():
                b_ = rr4["n"] % 6
                rr4["n"] += 1
                return b_

            def tile_ctx(t):
                par = t % 2
                return dict(t=t, par=par, tb=t // 4, tok=slice(t * 128, (t + 1) * 128), erev=erevR[par],
                            kdec=kdecR[par], vt=vR[par], gs=gsR[par], zt=zR[par], po2=6,
                            lT=None)

            def st_gate2(t):
                pp = (t // 2) % 2
                l3, lT2 = l32[pp], lR[pp]
                for j in range(2):
                    tt_ = t + j
                    tok = slice(tt_ * 128, (tt_ + 1) * 128)
                    p1 = nb4()
                    S.add("pe", lambda e, p1=p1, tok=tok: e.matmul(
                        bank(p1), lhsT=codeT[0:17, tok], rhs=gkup_b[:, :], start=True, stop=True),
                        r=["codeT", ("codeT", tt_ // 4), "gkup_b"], w=[("ps", p1)])
                    S.add("act", lambda e, p1=p1, j=j: e.activation(out=l3[:, j, :], in_=bank(p1), func=AF.Exp, scale=-1.0),
                          r=[("ps", p1)], w=[("l32", pp), ("ps", p1)])
                S.add("act", lambda e: e.activation(out=lT2, in_=l3, func=AF.Ln, bias=1.0),
                      r=[("l32", pp)], w=[("l", pp)])

            def st_rev(c):
                t, par, erev = c["t"], c["par"], c["erev"]
                pp = (t // 2) % 2
                lT = lR[pp][:, t % 2, :]
                p2 = nb4()
                S.add("pe", lambda e: e.matmul(bank(p2), lhsT=revtri_b[:, :], rhs=lT, start=True, stop=True),
                      r=[("l", pp), "revtri_b"], w=[("ps", p2)])
                S.add("act", lambda e: e.activation(out=erev, in_=bank(p2), func=AF.Exp, scale=-1.0 / 16.0),
                      r=[("ps", p2)], w=[("erev", par), ("ps", p2)])
                p4 = nb4()
                for h in range(4):
                    S.add("pe", lambda e, h=h: e.matmul(
                        bank(p4, 2, h * 2), lhsT=lT[:, h * 128:(h + 1) * 128], rhs=cind_b[:, :], start=True, stop=True),
                        r=[("l", pp), "cind_b"], w=[("ps", p4)])
                S.add("act", lambda e: e.activation(
                    out=decT[:, t, :, :], in_=bank(p4, 8).rearrange("p (h c) -> p h c", h=4), func=AF.Exp, scale=-1.0 / 16.0),
                    r=[("ps", p4)], w=[("decT", t), ("ps", p4)])

            def st_kproj(c):
                t, par, tok, erev, kdec = c["t"], c["par"], c["tok"], c["erev"], c["kdec"]
                Wk_l = Wk
                p3 = nb4()
                for kc in range(KC):
                    S.add("pe", lambda e, kc=kc: e.matmul(
                        bank(p3), lhsT=hT[:, kc, tok], rhs=Wk_l[:, kc, :], start=(kc == 0), stop=(kc == KC - 1)),
                        r=[("wb", wk_i), ("hT", t)], w=[("ps", p3)])
                S.add("dve", lambda e: e.tensor_tensor(out=kdec, in0=bank(p3), in1=erev, op=ALU.mult),
                      r=[("ps", p3), ("erev", par)], w=[("kdec", par), ("ps", p3)])

            def st_vproj(c):
                t, par, tok, vt = c["t"], c["par"], c["tok"], c["vt"]
                Wv_l = list(Wv)
                for half in range(2):
                    pvb = nb4()
                    for kc in range(KC):
                        S.add("pe", lambda e, kc=kc, pvb=pvb, half=half: e.matmul(
                            bank(pvb), lhsT=hT[:, kc, tok], rhs=Wv_l[half][:, kc, :], start=(kc == 0), stop=(kc == KC - 1)),
                            r=[("wb", wv_i[half]), ("hT", t)], w=[("ps", pvb)])
                    S.add("dve", lambda e, half=half, pvb=pvb: e.tensor_copy(out=vt[:, half * 512:(half + 1) * 512], in_=bank(pvb)),
                          r=[("ps", pvb)], w=[("v", par, half), ("ps", pvb)])

            def st_gproj(c):
                t, par, tok, gs = c["t"], c["par"], c["tok"], c["gs"]
                Wg_l = list(Wg)
                for half in range(2):
                    pgb = nb4()
                    hs = slice(half * 512, (half + 1) * 512)
                    for kc in range(KC):
                        S.add("pe", lambda e, kc=kc, pgb=pgb, half=half: e.matmul(
                            bank(pgb), lhsT=hT[:, kc, tok], rhs=Wg_l[half][:, kc, :], start=(kc == 0), stop=(kc == KC - 1)),
                            r=[("wb", wgg_i[half]), ("hT", t)], w=[("ps", pgb)])
                    S.add("act", lambda e, hs=hs, pgb=pgb: e.activation(out=gs[:, hs], in_=bank(pgb), func=AF.Tanh, scale=0.5),
                          r=[("ps", pgb)], w=[("gs", par, half), ("ps", pgb)])
                    S.add("dve", lambda e, hs=hs, pgb=pgb: e.scalar_tensor_tensor(
                        out=gs[:, hs], in0=gs[:, hs], scalar=1.0, in1=bank(pgb), op0=ALU.add, op1=ALU.mult),
                        r=[("ps", pgb), ("gs", par, half)], w=[("gs", par, half), ("ps", pgb)])
                    S.add("pool", lambda e, hs=hs: e.tensor_tensor(out=gs[:, hs], in0=gs[:, hs], in1=gnb[:, hs], op=ALU.mult),
                          r=[("gs", par, half), "gnb"], w=[("gs", par, half)])

            def st_upd(c, ci):
                t, par, kdec, vt = c["t"], c["par"], c["kdec"], c["vt"]
                pus = [nb4(), nb4()]
                for h in range(4):
                    pbk = pus[h // 2]
                    S.add("pe", lambda e, h=h, pbk=pbk: e.matmul(
                        bank(pbk, 256, (h % 2) * 256), lhsT=kdec[64 * ci:64 * ci + 64, h * 128:(h + 1) * 128],
                        rhs=vt[64 * ci:64 * ci + 64, h * 256:(h + 1) * 256], start=True, stop=True),
                        r=[("kdec", par), ("v", par, h // 2)], w=[("ps", pbk)])
                for h in range(4):
                    pbk = pus[h // 2]
                    S.add("dve", lambda e, h=h, pbk=pbk: e.scalar_tensor_tensor(
                        out=Sst[:, h, :], in0=Sst[:, h, :], scalar=decT[:, t, h, ci:ci + 1],
                        in1=bank(pbk, 256, (h % 2) * 256), op0=ALU.mult, op1=ALU.add),
                        r=[("ps", pbk), ("decT", t), ("Sst", h)], w=[("Sst", h), ("ps", pbk)])
                S.add("act", lambda e: e.copy(out=Sb[ci], in_=Sst), r=[("Sst", h) for h in range(4)], w=[("Sb", ci)])

            def st_o(c, ci):
                t, tb, po2 = c["t"], c["tb"], c["po2"]
                ch = 2 * t + ci
                for h in range(4):
                    S.add("pe", lambda e, h=h: e.matmul(
                        ps[64 * ci:64 * ci + 64, (po2 + h // 2) * 512 + (h % 2) * 256:(po2 + h // 2) * 512 + (h % 2) * 256 + 256],
                        lhsT=qgT[:, h, ch * 64:(ch + 1) * 64], rhs=Sb[ci][:, h, :], start=True, stop=True),
                        r=[("Sb", ci), ("qgT", tb)], w=[("ps", po2 + h // 2)])

            def st_norm(c):
                t, par, po2, gs, zt = c["t"], c["par"], c["po2"], c["gs"], c["zt"]
                cst = stat_col()
                ssq = stat[:, cst:cst + 4]
                kst = ("stat", cst)
                osrc = ps[:, po2 * 512:(po2 + 2) * 512].rearrange("p (h d) -> p h d", h=4)
                for h in range(4):
                    pk = ("ps", po2 + h // 2)
                    S.add("act", lambda e, h=h: e.activation(
                        out=zt[:, h * 256:(h + 1) * 256], in_=osrc[:, h, :], func=AF.Square, accum_out=ssq[:, h:h + 1]),
                        r=[pk], w=[kst, ("z", par), pk])
                S.add("dve", lambda e: e.tensor_scalar(out=ssq, in0=ssq, scalar1=1.0 / 256.0, scalar2=EPS,
                                                       op0=ALU.mult, op1=ALU.add), r=[kst], w=[kst])
                S.add("pool", lambda e: e.tensor_tensor(out=ssq, in0=ssq, in1=neghalf[:, 0:4], op=ALU.pow),
                      r=[kst, "neghalf"], w=[kst])

                def zmul():
                    for h in range(4):
                        pk = ("ps", po2 + h // 2)
                        S.add("dve", lambda e, h=h: e.scalar_tensor_tensor(
                            out=zt[:, h * 256:(h + 1) * 256], in0=osrc[:, h, :], scalar=ssq[:, h:h + 1],
                            in1=gs[:, h * 256:(h + 1) * 256], op0=ALU.mult, op1=ALU.mult),
                            r=[pk, kst, ("gs", par, h // 2)], w=[("z", par), pk])
                return zmul

            def st_tr(c):
                t, par, tok, zt = c["t"], c["par"], c["tok"], c["zt"]
                for half in range(2):
                    ptr = nb4()
                    for j in range(4):
                        kc = half * 4 + j
                        S.add("pe", lambda e, kc=kc, j=j, ptr=ptr: e.matmul(
                            bank(ptr, 128, j * 128), lhsT=zt[:, kc * 128:(kc + 1) * 128], rhs=ident[:],
                            start=True, stop=True),
                            r=[("z", par), "ident"], w=[("ps", ptr)])
                    src = bank(ptr).rearrange("p (a b) -> p a b", a=4)
                    dst = zgT[:, half * 4:(half + 1) * 4, tok]
                    if half == 0:
                        S.add("act", lambda e, src=src, dst=dst: e.copy(out=dst, in_=src),
                              r=[("ps", ptr)], w=[("zgT", t), ("ps", ptr)])
                    else:
                        S.add("dve", lambda e, src=src, dst=dst: e.tensor_copy(out=dst, in_=src),
                              r=[("ps", ptr)], w=[("zgT", t), ("ps", ptr)])

            for it in range(NTT + 3):
                c0 = tile_ctx(it) if it < NTT else None
                c1 = tile_ctx(it - 1) if 0 <= it - 1 < NTT else None
                c2 = tile_ctx(it - 2) if 0 <= it - 2 < NTT else None
                c3 = tile_ctx(it - 3) if 0 <= it - 3 < NTT else None
                zmul = st_norm(c3) if c3 else None
                if c0 and it % 2 == 0:
                    st_gate2(it)
                if c2:
                    st_upd(c2, 0)
                if zmul:
                    zmul()
                if c2:
                    st_upd(c2, 1)
                if c1:
                    st_rev(c1)
                    st_kproj(c1)
                    st_vproj(c1)
                if c2:
                    st_o(c2, 0)
                    st_o(c2, 1)
                    st_gproj(c2)
                if c3:
                    st_tr(c3)

            for dpos in range(5):
                wrelease(wpos_k + dpos)

            if "zgT" in dbg_d and seq == 0:
                S.barrier()
                D2 = Arena()
                D2.off = 56 * 1024
                tmp = D2.take([2048], F32)
                for kc in range(KC):
                    S.add("dve", lambda e, kc=kc: e.tensor_copy(out=tmp, in_=zgT[:, kc, :]),
                          r=[("zgT", t) for t in range(NTT)], w=["dbgtmp"])
                    S.add("sp", lambda e, kc=kc: e.dma_start(out=dbg_d["zgT"][:, kc * T:(kc + 1) * T], in_=tmp),
                          r=["dbgtmp"], dma="dbg")


        if "D" in phases:
            S.barrier()
            Dn = Arena()
            mT = Dn.take([KC, T], BF16)
            tgR = [Dn.take([2, 512], F32) for _ in range(2)]
            t1R = [Dn.take([512], F32) for _ in range(1)]
            xtD = [Dn.take([1024], F32) for _ in range(2)]
            yoD = [Dn.take([1024], F32) for _ in range(2)]
            gpost = Dn.take([1024], F32)
            sqD = [Dn.take([1024], BF16) for _ in range(2)]
            rr = {"n": 0}

            def nb(k=1):
                if k == 2 and rr["n"] % 2:
                    rr["n"] += 1
                b_ = rr["n"] % 8
                rr["n"] += k
                return b_

            hT_all = [("hT", t) for t in range(NTT)]
            zg_all = [("zgT", t) for t in range(NTT)]
            S.add("sp", lambda e: e.dma_start(out=gpost, in_=vecs_d[1:2, :].partition_broadcast(128)),
                  w=["gpost"], dma="c_gpost")
            it = 0
            for dc in range(8):
                wi, wpos = wnext(G_D0 + dc)
                W = wb[wi]
                kw = ("wb", wi)
                for tb in range(NTB):
                    tsl = slice(tb * 512, (tb + 1) * 512)
                    tg = tgR[it % 2]
                    t1 = t1R[0]
                    ktg = ("tgD", it % 2)
                    it += 1
                    pbs = []
                    for j, (src, rkeys) in enumerate(((hT, hT_all[tb * 4:tb * 4 + 4]), (hT, hT_all[tb * 4:tb * 4 + 4]),
                                                     (zgT, zg_all[tb * 4:tb * 4 + 4]), (zaT, [("zaT", tb // 2)]))):
                        pb = nb()
                        pbs.append(pb)
                        for kc in range(KC):
                            S.add("pe", lambda e, pb=pb, kc=kc, j=j, src=src, W=W, tsl=tsl: e.matmul(
                                bank(pb), lhsT=W[:, kc, j * 128:(j + 1) * 128], rhs=src[:, kc, tsl],
                                start=(kc == 0), stop=(kc == KC - 1)),
                                r=[kw] + rkeys, w=[("ps", pb)])
                        if j < 2:
                            col = dc if j == 0 else 8 + dc
                            S.add("act", lambda e, pb=pb, j=j, tg=tg, col=col: e.activation(
                                out=tg[:, j, :], in_=bank(pb), func=AF.Tanh, bias=mbias[:, col:col + 1], scale=0.5),
                                r=[("ps", pb), "mbias"], w=[ktg, ("ps", pb)])
                    S.add("dve", lambda e, tg=tg, t1=t1, pb=pbs[2]: e.scalar_tensor_tensor(
                        out=t1, in0=tg[:, 0, :], scalar=1.0, in1=bank(pb), op0=ALU.add, op1=ALU.mult),
                        r=[ktg, ("ps", pbs[2])], w=["t1", ("ps", pbs[2])])
                    S.add("dve", lambda e, tg=tg, pb=pbs[3]: e.scalar_tensor_tensor(
                        out=tg[:, 1, :], in0=tg[:, 1, :], scalar=1.0, in1=bank(pb), op0=ALU.add, op1=ALU.mult),
                        r=[ktg, ("ps", pbs[3])], w=[ktg, ("ps", pbs[3])])
                    S.add("pool", lambda e, tg=tg, t1=t1, dc=dc, tsl=tsl: e.tensor_tensor(
                        out=mT[:, dc, tsl], in0=t1, in1=tg[:, 1, :], op=ALU.add),
                        r=["t1", ktg], w=[("mT", tb)])
                wrelease(wpos)
            wo0, wpos_o = wnext(G_O0)
            wo_i = [wo0, wnext(G_O0 + 1)[0]]
            Wo_l = [wb[wo_i[0]], wb[wo_i[1]]]

            def d_mm(tt, seq=seq):
                par = tt % 2
                tok = slice(tt * 128, (tt + 1) * 128)
                xi, sq = xtD[par], sqD[par]
                kx = ("xtD", par)
                S.add("sp", lambda e: e.dma_start(out=xi, in_=x_d[seq, tok, :]), w=[kx], dma="xtD%d" % par)
                py = nb(2)
                for half in range(2):
                    for kc in range(KC):
                        S.add("pe", lambda e, half=half, kc=kc: e.matmul(
                            bank(py + half), lhsT=mT[:, kc, tok], rhs=Wo_l[half][:, kc, :],
                            start=(kc == 0), stop=(kc == KC - 1)),
                            r=[("wb", wo_i[half]), ("mT", tt // 4)], w=[("ps", py + half)])
                ysrc = ps[:, py * 512:(py + 2) * 512]
                pyk = [("ps", py), ("ps", py + 1)]
                cst = stat_col()
                ssq = stat[:, cst:cst + 1]
                kst = ("stat", cst)
                S.add("act", lambda e: e.activation(out=sq, in_=ysrc, func=AF.Square, accum_out=ssq),
                      r=pyk, w=[kst, ("sqD", par)] + pyk)
                S.add("dve", lambda e: e.tensor_scalar(out=ssq, in0=ssq, scalar1=1.0 / D, scalar2=4.0 * EPS,
                                                       op0=ALU.mult, op1=ALU.add), r=[kst], w=[kst])
                S.add("pool", lambda e: e.tensor_tensor(out=ssq, in0=ssq, in1=neghalf[:, 0:1], op=ALU.pow),
                      r=[kst, "neghalf"], w=[kst])
                return ysrc, pyk, ssq, kst

            def d_fin(tt, ysrc, pyk, ssq, kst, seq=seq):
                par = tt % 2
                tok = slice(tt * 128, (tt + 1) * 128)
                xi, yo = xtD[par], yoD[par]
                kx = ("xtD", par)
                S.add("dve", lambda e: e.scalar_tensor_tensor(
                    out=yo, in0=ysrc, scalar=ssq, in1=gpost, op0=ALU.mult, op1=ALU.mult),
                    r=pyk + [kst, "gpost"], w=[("yoD", par)] + pyk)
                S.add("pool", lambda e: e.tensor_tensor(out=xi, in0=xi, in1=yo, op=ALU.add),
                      r=[kx, ("yoD", par)], w=[kx])
                S.add("sp", lambda e: e.dma_start(out=out_d[seq, tok, :], in_=xi), r=[kx], dma="oD%d" % par)

            d_st = {}
            for it in range(NTT + 1):
                if it < NTT:
                    d_st[it] = d_mm(it)
                if it >= 1:
                    d_fin(it - 1, *d_st[it - 1])
            wrelease(wpos_o)
            wrelease(wpos_o + 1)

    w_try_issue(limit=2)
    for seq_ in range(nseq):
        do_seq(seq_)

    S.finalize()
    sem_names = list(S.ENGS[:4]) + S.dma_sems
    import contextlib
    with contextlib.ExitStack() as es:
        sems = {n: es.enter_context(nc.semaphore("s_" + str(n))) for n in sem_names}
        with nc.Block() as block:
            @block.tensor
            def _(e):
                S.emit("pe", e, sems)

            @block.scalar
            def _(e):
                S.emit("act", e, sems)

            @block.vector
            def _(e):
                S.emit("dve", e, sems)

            @block.gpsimd
            def _(e):
                S.emit("pool", e, sems)

            @block.sync
            def _(e):
                S.emit("sp", e, sems)
                for n in S.dma_sems:
                    e.wait_ge(sems[n], S.final_counts[n])
    return nc


def _prep_shared(norm_pre_g, w_in, gk_up, gk_bias, gla_norm_g, rel_bias, w_o_gla, w_o_att,
                 merge_bias, w_out, norm_post_g):
    w_in = np.asarray(w_in, np.float32)
    o_qg, o_kg, o_vg, o_gg, o_code = 0, 512, 1024, 2048, 3072
    o_qa, o_ka, o_va, o_ga, o_gate = 3088, 4112, 5136, 6160, 7184
    groups = []
    for hp in range(8):
        cols = np.concatenate([np.arange(o + hp * 128, o + hp * 128 + 128) for o in (o_qa, o_ka, o_va, o_ga)])
        groups.append(w_in[:, cols])
    groups.append(w_in[:, o_qg:o_qg + 512])
    groups.append(w_in[:, o_kg:o_kg + 512])
    groups.append(w_in[:, o_vg:o_vg + 512])
    groups.append(w_in[:, o_vg + 512:o_vg + 1024])
    groups.append(w_in[:, o_gg:o_gg + 512])
    groups.append(w_in[:, o_gg + 512:o_gg + 1024])
    w_o_gla = np.asarray(w_o_gla, np.float32)
    w_o_att = np.asarray(w_o_att, np.float32)
    w_out = np.asarray(w_out, np.float32)
    for dc in range(8):
        sl = slice(dc * 128, dc * 128 + 128)
        groups.append(np.concatenate([w_in[:, o_gate + dc * 128:o_gate + dc * 128 + 128],
                                      w_in[:, o_gate + 1024 + dc * 128:o_gate + 1024 + dc * 128 + 128],
                                      w_o_gla[:, sl], w_o_att[:, sl]], axis=1))
    groups.append(w_out[:, 0:512])
    groups.append(w_out[:, 512:1024])
    wg = np.stack(groups, 0)
    wg = wg.reshape(NGROUPS, KC, 128, 512).transpose(0, 2, 1, 3).reshape(NGROUPS, 128, KC * 512)
    wg = np.ascontiguousarray(wg)
    wcode = w_in[:, o_code:o_code + 16].reshape(KC, 128, 16).transpose(1, 0, 2).reshape(128, KC * 16)
    gkup = np.concatenate([np.asarray(gk_up, np.float32), np.asarray(gk_bias, np.float32)[None, :]], 0)
    vecs = np.stack([np.asarray(norm_pre_g, np.float32), np.asarray(norm_post_g, np.float32),
                     np.tile(np.asarray(gla_norm_g, np.float32), 4)], 0)
    mb = np.asarray(merge_bias, np.float32).reshape(16, 128).T
    kk = np.arange(128)[:, None]
    qq = np.arange(640)[None, :]
    relb = np.asarray(rel_bias, np.float32)[:, np.clip(qq - kk, -256, 256) + 256]
    ident = np.eye(128, dtype=np.float32)
    tp = np.arange(128)
    revtri = ((tp[:, None] > tp[None, :]) & (tp[:, None] // 64 == tp[None, :] // 64)).astype(np.float32)
    cind = (tp[:, None] // 64 == np.arange(2)[None, :]).astype(np.float32)
    return {
        "wg": wg, "wcode": np.ascontiguousarray(wcode), "gkup": np.ascontiguousarray(gkup),
        "vecs": np.ascontiguousarray(vecs), "mbias": np.ascontiguousarray(mb),
        "relb": np.ascontiguousarray(relb), "ident": ident, "revtri": revtri, "cind": cind,
    }


_NC_CACHE = {}


def kernel(x, norm_pre_g, w_in, gk_up, gk_bias, gla_norm_g, rel_bias, w_o_gla, w_o_att,
           merge_bias, w_out, norm_post_g, _debug=None, _nseq=NSEQ, _phases="ABCD"):
    x = np.asarray(x, np.float32)
    shared = _prep_shared(norm_pre_g, w_in, gk_up, gk_bias, gla_norm_g, rel_bias, w_o_gla, w_o_att,
                          merge_bias, w_out, norm_post_g)
    nc = build_nc(debug=_debug, nseq=_nseq, phases=_phases)
    in_maps = []
    for c in range(N_CORES):
        m = dict(shared)
        m["x"] = np.ascontiguousarray(x[2 * c:2 * c + 2])
        in_maps.append(m)
    res = run_bass_kernel_spmd(nc, in_maps, core_ids=list(range(N_CORES)))
    out = np.concatenate([np.asarray(r["out"]) for r in res.results], axis=0).astype(np.float32)
    if _debug is not None:
        return out, res.results
    return out
```

```python
import os
import numpy as np
import ml_dtypes
import concourse.bass as bass
import concourse.mybir as mybir
from concourse.bass_utils import run_bass_kernel_spmd

F32 = mybir.dt.float32
BF16 = mybir.dt.bfloat16
U8 = mybir.dt.uint8
AF = mybir.ActivationFunctionType
ALU = mybir.AluOpType
AX = mybir.AxisListType

N_CORES = 8
D = 1024
T = 2048
NSEQ = 2
KC = 8
NTT = T // 128
NTB = T // 512
EPS = 1e-6
NGROUPS = 24
G_B0, G_Q, G_K, G_V0, G_V1, G_G0, G_G1, G_D0, G_O0 = 0, 8, 9, 10, 11, 12, 13, 14, 22
NEG = -30000.0


class Op:
    __slots__ = ("eng", "fn", "deps", "idx", "signal", "count", "sem", "is_dma", "waits", "done")

    def __init__(self, eng, fn, is_dma, sem):
        self.eng = eng
        self.fn = fn
        self.deps = set()
        self.signal = False
        self.count = 0
        self.sem = sem
        self.is_dma = is_dma
        self.waits = []
        self.done = None


class Sched:
    ENGS = ("pe", "act", "dve", "pool", "sp")

    def __init__(self):
        self.ops = []
        self.per_eng = {e: [] for e in self.ENGS}
        self.last_w = {}
        self.readers = {}
        self.dma_sems = []
        self.last_by_sem = {}
        self.bar_deps = []
        self.bar_seen = set()

    def barrier(self):
        self.bar_deps = [o for s, o in self.last_by_sem.items() if not (isinstance(s, str) and s.startswith("w"))]
        self.bar_seen = set()

    def add(self, eng, fn, r=(), w=(), dma=None):
        is_dma = dma is not None
        op = Op(eng, fn, is_dma, dma if is_dma else eng)
        if is_dma and dma not in self.dma_sems:
            self.dma_sems.append(dma)
        op.idx = len(self.ops)
        deps = op.deps
        if self.bar_deps and eng != "pe" and eng not in self.bar_seen:
            self.bar_seen.add(eng)
            for o in self.bar_deps:
                deps.add(o)
        self.last_by_sem[op.sem] = op
        for k in r:
            lw = self.last_w.get(k)
            if lw is not None:
                deps.add(lw)
        for k in w:
            lw = self.last_w.get(k)
            if lw is not None and (lw.eng != eng or lw.is_dma or is_dma or eng != 'pe'):
                deps.add(lw)
            rd = self.readers.get(k)
            if rd:
                for o in rd.values():
                    if o.eng != eng or o.is_dma or is_dma or eng != 'pe':
                        deps.add(o)
        for k in r:
            self.readers.setdefault(k, {})[(eng, op.idx) if is_dma else eng] = op
        for k in w:
            self.last_w[k] = op
            self.readers[k] = {}
        self.ops.append(op)
        self.per_eng[eng].append(op)
        return op

    def finalize(self):
        for op in self.ops:
            for d in op.deps:
                d.signal = True
        cnt = {}
        for op in self.ops:
            if op.is_dma:
                cnt[op.sem] = cnt.get(op.sem, 0) + 16
                op.count = cnt[op.sem]
            elif op.signal:
                cnt[op.sem] = cnt.get(op.sem, 0) + 1
                op.count = cnt[op.sem]
        self.final_counts = cnt
        clocks = {e: {} for e in self.ENGS}
        for op in self.ops:
            clk = clocks[op.eng]
            waits = {}
            for d in sorted(op.deps, key=lambda o: o.idx):
                if clk.get(d.sem, 0) < d.count:
                    waits[d.sem] = max(waits.get(d.sem, 0), d.count)
                    for s, v in d.done.items():
                        if clk.get(s, 0) < v:
                            clk[s] = v
            op.waits = list(waits.items())
            done = dict(clk)
            if op.is_dma or op.signal:
                if done.get(op.sem, 0) < op.count:
                    done[op.sem] = op.count
            op.done = done

    def emit(self, name, eng, sems):
        for op in self.per_eng[name]:
            for s, v in op.waits:
                eng.wait_ge(sems[s], v)
            ins = op.fn(eng)
            if op.is_dma:
                ins.then_inc(sems[op.sem], 16)
            elif op.signal:
                ins.then_inc(sems[op.sem], 1)


def build_nc(debug=None, nseq=NSEQ, phases="ABCD"):
    debug = debug or {}
    nc = bass.Bass("TRN2", target_bir_lowering=False)
    S = Sched()

    x_d = nc.dram_tensor("x", [NSEQ, T, D], F32, kind="ExternalInput")
    wg_d = nc.dram_tensor("wg", [NGROUPS, 128, KC * 512], F32, kind="ExternalInput")
    wcode_d = nc.dram_tensor("wcode", [128, KC * 16], F32, kind="ExternalInput")
    gkup_d = nc.dram_tensor("gkup", [17, 512], F32, kind="ExternalInput")
    vecs_d = nc.dram_tensor("vecs", [3, 1024], F32, kind="ExternalInput")
    mbias_d = nc.dram_tensor("mbias", [128, 16], F32, kind="ExternalInput")
    relb_d = nc.dram_tensor("relb", [16, 128, 640], F32, kind="ExternalInput")
    ident_d = nc.dram_tensor("ident", [128, 128], F32, kind="ExternalInput")
    revtri_d = nc.dram_tensor("revtri", [128, 128], F32, kind="ExternalInput")
    cind_d = nc.dram_tensor("cind", [128, 2], F32, kind="ExternalInput")
    out_d = nc.dram_tensor("out", [NSEQ, T, D], F32, kind="ExternalOutput")
    dbg_d = {}
    for name, shape in debug.items():
        dbg_d[name] = nc.dram_tensor("dbg_" + name, list(shape), F32, kind="ExternalOutput")

    hT = nc.alloc_sbuf_tensor("hT", [128, KC, T], BF16)
    zaT = nc.alloc_sbuf_tensor("zaT", [128, KC, T], BF16)
    zgT = nc.alloc_sbuf_tensor("zgT", [128, KC, T], BF16)
    NWB = 5
    wb = [nc.alloc_sbuf_tensor("wb%d" % i, [128, KC, 512], BF16) for i in range(NWB)]
    ident = nc.alloc_sbuf_tensor("ident_bf", [128, 128], BF16)
    identf = nc.alloc_sbuf_tensor("ident_f", [128, 128], F32)
    revtri = nc.alloc_sbuf_tensor("revtri_sb", [128, 128], F32)
    cind = nc.alloc_sbuf_tensor("cind_sb", [128, 2], F32)
    mbias = nc.alloc_sbuf_tensor("mbias_sb", [128, 16], F32)
    gkup = nc.alloc_sbuf_tensor("gkup_sb", [17, 512], F32)
    wcode = nc.alloc_sbuf_tensor("wcode_sb", [128, KC, 16], BF16)
    neghalf = nc.alloc_sbuf_tensor("neghalf", [128, 8], F32)
    revtri_b = nc.alloc_sbuf_tensor("revtri_b", [128, 128], BF16)
    cind_b = nc.alloc_sbuf_tensor("cind_b", [128, 2], BF16)
    gkup_b = nc.alloc_sbuf_tensor("gkup_b", [17, 512], BF16)
    stat = nc.alloc_sbuf_tensor("stat", [128, 64], F32)
    ARENA_BYTES = 68288
    arena = nc.alloc_sbuf_tensor("arena", [128, ARENA_BYTES], U8)
    ps = nc.alloc_psum_tensor("ps", [128, 4096], F32)
    ps_bf = ps.bitcast(BF16)

    class Arena:
        def __init__(self):
            self.off = 0

        def take(self, free_shape, dtype):
            n = int(np.prod(free_shape)) * mybir.dt.size(dtype)
            n_al = (n + 31) // 32 * 32
            assert self.off + n_al <= ARENA_BYTES, ("arena overflow", self.off, n_al)
            ap = arena[:, self.off:self.off + n].bitcast(dtype)
            self.off += n_al
            if len(free_shape) == 2:
                ap = ap.rearrange("p (a b) -> p a b", a=free_shape[0])
            elif len(free_shape) == 3:
                ap = ap.rearrange("p (a b c) -> p a b c", a=free_shape[0], b=free_shape[1])
            elif len(free_shape) == 4:
                ap = ap.rearrange("p (a b c d) -> p a b c d", a=free_shape[0], b=free_shape[1], c=free_shape[2])
            return ap

    def bank(b, n=512, off=0):
        return ps[:, b * 512 + off: b * 512 + off + n]

    def bank_bf(b, n=1024, off=0):
        return ps_bf[:, b * 1024 + off: b * 1024 + off + n]

    def cload(dst_ap, src_ap, key, tmp=None):
        S.add("sp", lambda e, d=dst_ap, s=src_ap: e.dma_start(out=d, in_=s), w=[key], dma="c_" + str(key))

    cload(identf[:], ident_d.ap(), "identf")
    cload(revtri[:], revtri_d.ap(), "revtri")
    cload(cind[:], cind_d.ap(), "cind")
    cload(mbias[:], mbias_d.ap(), "mbias")
    cload(gkup[:], gkup_d.ap(), "gkup")
    S.add("dve", lambda e: e.tensor_copy(out=ident[:], in_=identf[:]), r=["identf"], w=["ident"])
    S.add("pool", lambda e: e.memset(neghalf[:], -0.5), w=["neghalf"])
    S.add("dve", lambda e: e.tensor_copy(out=revtri_b[:], in_=revtri[:]), r=["revtri"], w=["revtri_b"])
    S.add("dve", lambda e: e.tensor_copy(out=cind_b[:], in_=cind[:]), r=["cind"], w=["cind_b"])
    S.add("dve", lambda e: e.tensor_copy(out=gkup_b[:], in_=gkup[:]), r=["gkup"], w=["gkup_b"])
    S.add("dve", lambda e: e.tensor_scalar(out=mbias[:], in0=mbias[:], scalar1=0.5, scalar2=None, op0=ALU.mult),
          r=["mbias"], w=["mbias"])
    S.add("pool", lambda e: e.dma_start(out=wcode[:].rearrange("p a b -> p (a b)"), in_=wcode_d.ap()),
          w=["wcode"], dma="c_wcode")

    per_seq = []
    if "B" in phases:
        per_seq += [G_B0 + i for i in range(8)]
    if "C" in phases:
        per_seq += [G_Q, G_K, G_V0, G_V1, G_G0, G_G1]
    if "D" in phases:
        per_seq += [G_D0 + i for i in range(8)] + [G_O0, G_O0 + 1]
    wsched = per_seq * nseq
    wstate = {"issued": 0, "pos": 0}
    released = set()

    def w_try_issue(limit=None):
        while wstate["issued"] < (len(wsched) if limit is None else min(limit, len(wsched))) and (wstate["issued"] < NWB or (wstate["issued"] - NWB) in released):
            j = wstate["issued"]
            i = j % NWB
            g = wsched[j]
            S.add("pool", lambda e, i=i, g=g: e.dma_start(out=wb[i][:].rearrange("p a b -> p (a b)"), in_=wg_d[g]),
                  w=[("wb", i)], dma="w%d" % i)
            wstate["issued"] += 1

    def wnext(expect_g):
        pos = wstate["pos"]
        assert wsched[pos] == expect_g, (pos, wsched[pos], expect_g)
        w_try_issue()
        assert wstate["issued"] > pos, ("weight slot not released in time", pos)
        wstate["pos"] += 1
        return pos % NWB, pos

    def wrelease(pos):
        released.add(pos)
        w_try_issue()

    stat_n = {"n": 0}

    def stat_col(n=1):
        c = stat_n["n"] % (64 // 4) * 4
        stat_n["n"] += 1
        return c

    def do_seq(seq):
        S.barrier()
        A = Arena()
        gpre = A.take([1024], F32)
        NXT = 8
        xt = [A.take([1024], F32) for _ in range(NXT)]
        sqr = [A.take([1024], BF16) for _ in range(2)]
        NHB = 4
        hb = [A.take([1024], BF16) for _ in range(NHB)]
        S.add("sp", lambda e, d=gpre: e.dma_start(out=d, in_=vecs_d[0:1, :].partition_broadcast(128)),
              w=["gpre"], dma="c_gpre")
        def a_load(tt, seq=seq):
            xi = xt[tt % NXT]
            kx = ("xt", tt % NXT)
            S.add("sp", lambda e: e.dma_start(out=xi, in_=x_d[seq, tt * 128:(tt + 1) * 128, :]),
                  w=[kx], dma="xt%d" % (tt % NXT))
            c = stat_col()
            ssq = stat[:, c:c + 1]
            ks = ("stat", c)
            S.add("act", lambda e, sq=sqr[tt % 2]: e.activation(
                out=sq, in_=xi, func=AF.Square, accum_out=ssq), r=[kx], w=[ks, ("sq", tt % 2)])
            S.add("dve", lambda e: e.tensor_scalar(out=ssq, in0=ssq, scalar1=1.0 / D, scalar2=EPS,
                                                   op0=ALU.mult, op1=ALU.add), r=[ks], w=[ks])
            S.add("pool", lambda e: e.tensor_tensor(out=ssq, in0=ssq, in1=neghalf[:, 0:1], op=ALU.pow),
                  r=[ks, "neghalf"], w=[ks])
            return ssq, ks

        def a_norm(tt, ssq, ks):
            xi = xt[tt % NXT]
            hi = hb[tt % NHB]
            kx = ("xt", tt % NXT)
            kh = ("hb", tt % NHB)
            S.add("dve", lambda e: e.scalar_tensor_tensor(
                out=hi, in0=xi, scalar=ssq, in1=gpre, op0=ALU.mult, op1=ALU.mult),
                r=[kx, ks, "gpre"], w=[kh])
            pb = 2 * (tt % 4)
            for kc in range(KC):
                S.add("pe", lambda e, kc=kc: e.matmul(
                    bank(pb + kc // 4, 128, (kc % 4) * 128), lhsT=hi[:, kc * 128:(kc + 1) * 128], rhs=ident[:],
                    start=True, stop=True),
                    r=[kh, "ident"], w=[("ps", pb + kc // 4)])

        def a_copy(tt):
            pb = 2 * (tt % 4)
            kp = [("ps", pb), ("ps", pb + 1)]
            src = ps[:, pb * 512:(pb + 2) * 512].rearrange("p (a b) -> p a b", a=KC)
            dst = hT[:, :, tt * 128:(tt + 1) * 128]
            if tt % 2 == 0:
                S.add("act", lambda e: e.copy(out=dst, in_=src), r=kp, w=[("hT", tt)] + kp)
            else:
                S.add("dve", lambda e: e.tensor_copy(out=dst, in_=src), r=kp, w=[("hT", tt)] + kp)

        a_stats = {}
        for it in range(NTT + 2):
            if it < NTT:
                a_stats[it] = a_load(it)
            if 0 <= it - 1 < NTT:
                a_norm(it - 1, *a_stats[it - 1])
            if 0 <= it - 2 < NTT:
                a_copy(it - 2)

        if "hT" in dbg_d and seq == 0:
            S.barrier()
            D2 = Arena()
            D2.off = 32 * 1024
            tmp = D2.take([2048], F32)
            for kc in range(KC):
                S.add("dve", lambda e, kc=kc: e.tensor_copy(out=tmp, in_=hT[:, kc, :]),
                      r=[("hT", t) for t in range(NTT)], w=["dbgtmp"])
                S.add("sp", lambda e, kc=kc: e.dma_start(out=dbg_d["hT"][:, kc * T:(kc + 1) * T], in_=tmp),
                      r=["dbgtmp"], dma="dbg")


        if "B" in phases:
            S.barrier()
            B = Arena()
            qTb, kTb, Vaugb, gsAb = [], [], [], []
            for _p in range(2):
                qTb.append(B.take([T], BF16))
                kTb.append(B.take([T], BF16))
                Vaugb.append(B.take([NTT, 2, 65], BF16))
                gsAb.append(B.take([NTT, 128], BF16))
            tgA = [B.take([2, 128], F32) for _ in range(1)]
            eB = B.take([2, 640], F32)
            eBb = B.take([2, 640], BF16)
            sbp = [B.take([2, 640], BF16) for _ in range(2)]
            PT = [B.take([2, 640], BF16) for _ in range(6)]
            zall = B.take([NTT, 128], BF16)
            for p in range(2):
                S.add("pool", lambda e, p=p: e.memset(Vaugb[p][:, :, :, 64:65], 2.0), w=[("vaug_ones", p)])
            proj_ring = {"n": 0}
            smain_ring = {"n": 0}
            hT_all = [("hT", t) for t in range(NTT)]

            def proj_items(hp):
                p = hp % 2
                wi, wpos = wnext(G_B0 + hp)
                W = wb[wi]
                kw = ("wb", wi)
                qT, kT, Vaug, gsA = qTb[p], kTb[p], Vaugb[p], gsAb[p]
                items = []

                def fm(which, tb):
                    dstT = qT if which == 0 else kT
                    kd = ("qT", p, tb) if which == 0 else ("kT", p, tb)
                    pb = proj_ring["n"] % 2
                    proj_ring["n"] += 1
                    for kc in range(KC):
                        S.add("pe", lambda e, pb=pb, kc=kc: e.matmul(
                            bank(pb), lhsT=W[:, kc, which * 128:(which + 1) * 128],
                            rhs=hT[:, kc, tb * 512:(tb + 1) * 512], start=(kc == 0), stop=(kc == KC - 1)),
                            r=[kw] + hT_all[tb * 4:tb * 4 + 4], w=[("ps", pb)])
                    if which == 0:
                        S.add("act", lambda e, pb=pb: e.activation(
                            out=dstT[:, tb * 512:(tb + 1) * 512], in_=bank(pb), func=AF.Copy, scale=0.125),
                            r=[("ps", pb)], w=[kd, ("ps", pb)])
                    else:
                        S.add("dve", lambda e, pb=pb: e.tensor_copy(
                            out=dstT[:, tb * 512:(tb + 1) * 512], in_=bank(pb)),
                            r=[("ps", pb)], w=[kd, ("ps", pb)])

                def tm(tp):
                    pb = proj_ring["n"] % 2
                    proj_ring["n"] += 1
                    for j in range(2):
                        tt = 2 * tp + j
                        for kc in range(KC):
                            S.add("pe", lambda e, pb=pb, kc=kc, tt=tt, j=j: e.matmul(
                                bank(pb, 256, j * 256), lhsT=hT[:, kc, tt * 128:(tt + 1) * 128],
                                rhs=W[:, kc, 256:512], start=(kc == 0), stop=(kc == KC - 1)),
                                r=[kw, ("hT", tt)], w=[("ps", pb)])
                    pview = bank(pb).rearrange("p (a b) -> p a b", a=2)
                    vsrc = pview[:, :, 0:128].rearrange("p a (h d) -> p a h d", h=2)
                    S.add("dve", lambda e: e.tensor_copy(out=Vaug[:, 2 * tp:2 * tp + 2, :, 0:64], in_=vsrc),
                          r=[("ps", pb)], w=[("V", p, tp), ("ps", pb)])
                    tg = tgA[0]
                    S.add("act", lambda e: e.activation(out=tg, in_=pview[:, :, 128:256], func=AF.Tanh, scale=0.5),
                          r=[("ps", pb)], w=[("tgA", 0), ("ps", pb)])
                    S.add("dve", lambda e: e.scalar_tensor_tensor(
                        out=gsA[:, 2 * tp:2 * tp + 2, :], in0=tg, scalar=1.0, in1=pview[:, :, 128:256],
                        op0=ALU.add, op1=ALU.mult),
                        r=[("ps", pb), ("tgA", 0)], w=[("gsA", p, tp), ("ps", pb)])

                for tb in range(NTB):
                    items.append(lambda tb=tb: fm(0, tb))
                    items.append(lambda tb=tb: fm(1, tb))
                for tp in range(NTT // 2):
                    items.append(lambda tp=tp: tm(tp))
                items.append(lambda: wrelease(wpos))
                return items

            def qg_items():
                qgT_ = Arena().take([4, T], BF16)
                alias = ([("qT", 0, t) for t in range(NTB)] + [("kT", 0, t) for t in range(NTB)]
                         + [("V", 0, t) for t in range(NTT // 2)] + [("gsA", 0, t) for t in range(NTT // 2)]
                         + [("vaug_ones", 0)])
                wi, wpos_q = wnext(G_Q)
                W = wb[wi]
                kw = ("wb", wi)
                items = []

                def one(cb, tb, n_ev):
                    pb = proj_ring["n"] % 2
                    proj_ring["n"] += 1
                    for kc in range(KC):
                        S.add("pe", lambda e, kc=kc: e.matmul(
                            bank(pb), lhsT=W[:, kc, cb * 128:(cb + 1) * 128],
                            rhs=hT[:, kc, tb * 512:(tb + 1) * 512], start=(kc == 0), stop=(kc == KC - 1)),
                            r=[kw] + hT_all[tb * 4:tb * 4 + 4], w=[("ps", pb)])
                    dst = qgT_[:, cb, tb * 512:(tb + 1) * 512]
                    if n_ev % 2 == 0:
                        S.add("act", lambda e: e.activation(out=dst, in_=bank(pb), func=AF.Copy, scale=128 ** -0.5),
                              r=[("ps", pb)], w=[("qgT", tb), ("ps", pb)] + alias)
                    else:
                        S.add("dve", lambda e: e.tensor_scalar(
                            out=dst, in0=bank(pb), scalar1=128 ** -0.5, scalar2=None, op0=ALU.mult),
                            r=[("ps", pb)], w=[("qgT", tb), ("ps", pb)] + alias)

                n_ev = 0
                for cb in range(4):
                    for tb in range(NTB):
                        items.append(lambda cb=cb, tb=tb, n_ev=n_ev: one(cb, tb, n_ev))
                        n_ev += 1
                items.append(lambda: wrelease(wpos_q))
                return items

            pending = proj_items(0)
            for hp in range(8):
                p = hp % 2
                qT, kT, Vaug, gsA = qTb[p], kTb[p], Vaugb[p], gsAb[p]
                for f in pending:
                    f()
                if hp + 1 < 8:
                    pending = proj_items(hp + 1)
                elif "C" in phases:
                    pending = qg_items()
                else:
                    pending = []
                for h in range(2):
                    head = hp * 2 + h
                    S.add("sp", lambda e, h=h, head=head: e.dma_start(out=eB[:, h, :], in_=relb_d[head]),
                          w=[("eBl", h)], dma="eB%d" % h)
                    S.add("pool", lambda e, h=h: e.memset(eB[64:128, h, 0:64], NEG), r=[("eBl", h)], w=[("eBl", h)])
                    S.add("pool", lambda e, h=h: e.memset(eB[0:64, h, 576:640], NEG), r=[("eBl", h)], w=[("eBl", h)])
                S.add("act", lambda e: e.activation(out=eBb, in_=eB, func=AF.Exp),
                      r=[("eBl", 0), ("eBl", 1)], w=["eB"])

                def qk_step(kt, p=p, qT=qT, kT=kT):
                    nq = min(640, T - 128 * kt)
                    nmain = min(512, nq)
                    ntail = nq - nmain
                    sbi = kt % 2
                    sb = sbp[sbi]
                    rk = [("kT", p, kt // 4)] + [("qT", p, t) for t in range(kt // 4, min(NTB, (128 * kt + nq - 1) // 512 + 1))]
                    sms = []
                    for h in range(2):
                        sm = 2 + smain_ring["n"] % 3
                        smain_ring["n"] += 1
                        sms.append(sm)
                        lhsT = kT[64 * h:64 * h + 64, 128 * kt:128 * kt + 128]
                        S.add("pe", lambda e, sm=sm, lhsT=lhsT, h=h: e.matmul(
                            bank(sm, nmain), lhsT=lhsT, rhs=qT[64 * h:64 * h + 64, 128 * kt:128 * kt + nmain],
                            start=True, stop=True), r=rk, w=[("ps", sm)])
                    if ntail:
                        for h in range(2):
                            lhsT = kT[64 * h:64 * h + 64, 128 * kt:128 * kt + 128]
                            S.add("pe", lambda e, lhsT=lhsT, h=h: e.matmul(
                                bank(5 + h, ntail), lhsT=lhsT,
                                rhs=qT[64 * h:64 * h + 64, 128 * kt + 512:128 * kt + 512 + ntail],
                                start=True, stop=True), r=rk, w=[("ps", 5 + h)])
                        for h in range(2):
                            S.add("act", lambda e, h=h: e.activation(
                                out=sb[:, h, 512:512 + ntail], in_=bank(5 + h, ntail), func=AF.Exp),
                                r=[("ps", 5 + h)], w=[("sb", sbi, h), ("ps", 5 + h)])
                    for h in range(2):
                        S.add("act", lambda e, h=h, sm=sms[h]: e.activation(
                            out=sb[:, h, 0:nmain], in_=bank(sm, nmain), func=AF.Exp),
                            r=[("ps", sms[h])], w=[("sb", sbi, h), ("ps", sms[h])])
                    S.add("dve", lambda e: e.tensor_tensor(
                        out=PT[kt % 6][:, :, 0:nq], in0=sb[:, :, 0:nq], in1=eBb[:, :, 0:nq], op=ALU.mult),
                        r=[("sb", sbi, 0), ("sb", sbi, 1), "eB"], w=[("PT", kt % 6, 0), ("PT", kt % 6, 1)])

                def pv_step(qt, p=p, Vaug=Vaug, gsA=gsA):
                    pv = bank(7, 130).rearrange("p (h d) -> p h d", h=2)
                    kts = list(range(max(0, qt - 4), qt + 1))
                    for h in range(2):
                        for i, k2 in enumerate(kts):
                            S.add("pe", lambda e, h=h, k2=k2, i=i: e.matmul(
                                pv[:, h, :], lhsT=PT[k2 % 6][:, h, (qt - k2) * 128:(qt - k2 + 1) * 128],
                                rhs=Vaug[:, k2, h, :], start=(i == 0), stop=(i == len(kts) - 1)),
                                r=[("PT", k2 % 6, h), ("V", p, k2 // 2), ("vaug_ones", p)], w=[("ps", 7)])
                    c = stat_col()
                    rden = stat[:, c:c + 2]
                    S.add("dve", lambda e: e.reciprocal(out=rden, in_=pv[:, :, 64]),
                          r=[("ps", 7)], w=[("stat", c), ("ps", 7)])
                    for h in range(2):
                        S.add("dve", lambda e, h=h: e.scalar_tensor_tensor(
                            out=zall[:, qt, 64 * h:64 * h + 64], in0=pv[:, h, 0:64], scalar=rden[:, h:h + 1],
                            in1=gsA[:, qt, 64 * h:64 * h + 64], op0=ALU.mult, op1=ALU.mult),
                            r=[("ps", 7), ("stat", c), ("gsA", p, qt // 2)], w=[("zall", qt), ("ps", 7)])

                for step in range(NTT + 1):
                    if step < NTT:
                        qk_step(step)
                    if pending:
                        pending.pop(0)()
                    if step >= 1:
                        pv_step(step - 1)
                for qb in range(4):
                    pb = proj_ring["n"] % 2
                    proj_ring["n"] += 1
                    for j in range(4):
                        qt = qb * 4 + j
                        S.add("pe", lambda e, pb=pb, j=j, qt=qt: e.matmul(
                            bank(pb, 128, j * 128), lhsT=zall[:, qt, :], rhs=ident[:], start=True, stop=True),
                            r=[("zall", qt), "ident"], w=[("ps", pb)])
                    if qb % 2 == 0:
                        S.add("act", lambda e, pb=pb, qb=qb, hp=hp: e.copy(
                            out=zaT[:, hp, qb * 512:(qb + 1) * 512], in_=bank(pb)),
                            r=[("ps", pb)], w=[("zaT", qb // 2), ("ps", pb)])
                    else:
                        S.add("dve", lambda e, pb=pb, qb=qb, hp=hp: e.tensor_copy(
                            out=zaT[:, hp, qb * 512:(qb + 1) * 512], in_=bank(pb)),
                            r=[("ps", pb)], w=[("zaT", qb // 2), ("ps", pb)])

            for f in pending:
                f()
            pending = []

            if "zaT" in dbg_d and seq == 0:
                S.barrier()
                D2 = Arena()
                D2.off = 56 * 1024
                tmp = D2.take([2048], F32)
                for kc in range(KC):
                    S.add("dve", lambda e, kc=kc: e.tensor_copy(out=tmp, in_=zaT[:, kc, :]),
                          r=[("zaT", 0), ("zaT", 1)], w=["dbgtmp"])
                    S.add("sp", lambda e, kc=kc: e.dma_start(out=dbg_d["zaT"][:, kc * T:(kc + 1) * T], in_=tmp),
                          r=["dbgtmp"], dma="dbg")


        if "C" in phases:
            S.barrier()
            C = Arena()
            qgT = C.take([4, T], BF16)
            codeT = C.take([T], BF16)
            decT = C.take([NTT, 4, 2], F32)
            Sst = C.take([4, 256], F32)
            Sb = [C.take([4, 256], BF16) for _ in range(2)]
            gnb = C.take([1024], F32)
            lR = [C.take([2, 512], BF16) for _ in range(2)]
            l32 = [C.take([2, 512], F32) for _ in range(2)]
            erevR = [C.take([512], F32) for _ in range(2)]
            kdecR = [C.take([512], BF16) for _ in range(2)]
            vR = [C.take([1024], BF16) for _ in range(2)]
            gsR = [C.take([1024], F32) for _ in range(2)]
            zR = [C.take([1024], BF16) for _ in range(2)]
            rr = {"n": 0}

            def nb(k=1):
                if k == 2 and rr["n"] % 2:
                    rr["n"] += 1
                b_ = rr["n"] % 8
                rr["n"] += k
                return b_

            hT_all = [("hT", t) for t in range(NTT)]
            S.add("sp", lambda e: e.dma_start(out=gnb, in_=vecs_d[2:3, :].partition_broadcast(128)),
                  w=["gnb"], dma="c_gnb")
            S.add("dve", lambda e: e.tensor_scalar(out=gnb, in0=gnb, scalar1=0.5, scalar2=None, op0=ALU.mult),
                  r=["gnb"], w=["gnb"])
            S.add("pool", lambda e: e.memset(codeT[0:32, :], 1.0), w=["codeT"])
            S.add("pool", lambda e: e.memset(Sst, 0.0), w=[("Sst", h) for h in range(4)])
            if "B" not in phases:
                wi, wpos_q = wnext(G_Q)
                W = wb[wi]
                kw = ("wb", wi)
                n_ev = 0
                for cb in range(4):
                    for tb in range(NTB):
                        pb = nb()
                        for kc in range(KC):
                            S.add("pe", lambda e, pb=pb, kc=kc, tb=tb, cb=cb, W=W: e.matmul(
                                bank(pb), lhsT=W[:, kc, cb * 128:(cb + 1) * 128],
                                rhs=hT[:, kc, tb * 512:(tb + 1) * 512], start=(kc == 0), stop=(kc == KC - 1)),
                                r=[kw] + hT_all[tb * 4:tb * 4 + 4], w=[("ps", pb)])
                        dst = qgT[:, cb, tb * 512:(tb + 1) * 512]
                        if n_ev % 2 == 0:
                            S.add("act", lambda e, pb=pb, dst=dst: e.activation(
                                out=dst, in_=bank(pb), func=AF.Copy, scale=128 ** -0.5),
                                r=[("ps", pb)], w=[("qgT", tb), ("ps", pb)])
                        else:
                            S.add("dve", lambda e, pb=pb, dst=dst: e.tensor_scalar(
                                out=dst, in0=bank(pb), scalar1=128 ** -0.5, scalar2=None, op0=ALU.mult),
                                r=[("ps", pb)], w=[("qgT", tb), ("ps", pb)])
                        n_ev += 1
                wrelease(wpos_q)
            for tb in range(NTB):
                pb = nb()
                for kc in range(KC):
                    S.add("pe", lambda e, pb=pb, kc=kc, tb=tb: e.matmul(
                        ps[0:16, pb * 512:(pb + 1) * 512], lhsT=wcode[:, kc, :],
                        rhs=hT[:, kc, tb * 512:(tb + 1) * 512], start=(kc == 0), stop=(kc == KC - 1)),
                        r=["wcode"] + hT_all[tb * 4:tb * 4 + 4], w=[("ps", pb)])
                S.add("dve", lambda e, pb=pb, tb=tb: e.tensor_copy(
                    out=codeT[0:16, tb * 512:(tb + 1) * 512], in_=ps[0:16, pb * 512:(pb + 1) * 512]),
                    r=[("ps", pb), "codeT"], w=[("codeT", tb), ("ps", pb)])
            wk_i, wpos_k = wnext(G_K)
            wv_i = [wnext(G_V0)[0], wnext(G_V1)[0]]
            wgg_i = [wnext(G_G0)[0], wnext(G_G1)[0]]
            Wk = wb[wk_i]
            Wv = [wb[wv_i[0]], wb[wv_i[1]]]
            Wg = [wb[wgg_i[0]], wb[wgg_i[1]]]
            rr4 = {"n": 0}

            def nb4():
                b_ = rr4["n"] % 6
                rr4["n"] += 1
                return b_

            def tile_ctx(t):
                par = t % 2
                return dict(t=t, par=par, tb=t // 4, tok=slice(t * 128, (t + 1) * 128), erev=erevR[par],
                            kdec=kdecR[par], vt=vR[par], gs=gsR[par], zt=zR[par], po2=6,
                            lT=None)

            def st_gate2(t):
                pp = (t // 2) % 2
                l3, lT2 = l32[pp], lR[pp]
                for j in range(2):
                    tt_ = t + j
                    tok = slice(tt_ * 128, (tt_ + 1) * 128)
                    p1 = nb4()
                    S.add("pe", lambda e, p1=p1, tok=tok: e.matmul(
                        bank(p1), lhsT=codeT[0:17, tok], rhs=gkup_b[:, :], start=True, stop=True),
                        r=["codeT", ("codeT", tt_ // 4), "gkup_b"], w=[("ps", p1)])
                    S.add("act", lambda e, p1=p1, j=j: e.activation(out=l3[:, j, :], in_=bank(p1), func=AF.Exp, scale=-1.0),
                          r=[("ps", p1)], w=[("l32", pp), ("ps", p1)])
                S.add("act", lambda e: e.activation(out=lT2, in_=l3, func=AF.Ln, bias=1.0),
                      r=[("l32", pp)], w=[("l", pp)])

            def st_rev(c):
                t, par, erev = c["t"], c["par"], c["erev"]
                pp = (t // 2) % 2
                lT = lR[pp][:, t % 2, :]
                p2 = nb4()
                S.add("pe", lambda e: e.matmul(bank(p2), lhsT=revtri_b[:, :], rhs=lT, start=True, stop=True),
                      r=[("l", pp), "revtri_b"], w=[("ps", p2)])
                S.add("act", lambda e: e.activation(out=erev, in_=bank(p2), func=AF.Exp, scale=-1.0 / 16.0),
                      r=[("ps", p2)], w=[("erev", par), ("ps", p2)])
                p4 = nb4()
                for h in range(4):
                    S.add("pe", lambda e, h=h: e.matmul(
                        bank(p4, 2, h * 2), lhsT=lT[:, h * 128:(h + 1) * 128], rhs=cind_b[:, :], start=True, stop=True),
                        r=[("l", pp), "cind_b"], w=[("ps", p4)])
                S.add("act", lambda e: e.activation(
                    out=decT[:, t, :, :], in_=bank(p4, 8).rearrange("p (h c) -> p h c", h=4), func=AF.Exp, scale=-1.0 / 16.0),
                    r=[("ps", p4)], w=[("decT", t), ("ps", p4)])

            def st_kproj(c):
                t, par, tok, erev, kdec = c["t"], c["par"], c["tok"], c["erev"], c["kdec"]
                Wk_l = Wk
                p3 = nb4()
                for kc in range(KC):
                    S.add("pe", lambda e, kc=kc: e.matmul(
                        bank(p3), lhsT=hT[:, kc, tok], rhs=Wk_l[:, kc, :], start=(kc == 0), stop=(kc == KC - 1)),
                        r=[("wb", wk_i), ("hT", t)], w=[("ps", p3)])
                S.add("dve", lambda e: e.tensor_tensor(out=kdec, in0=bank(p3), in1=erev, op=ALU.mult),
                      r=[("ps", p3), ("erev", par)], w=[("kdec", par), ("ps", p3)])

            def st_vproj(c):
                t, par, tok, vt = c["t"], c["par"], c["tok"], c["vt"]
                Wv_l = list(Wv)
                for half in range(2):
                    pvb = nb4()
                    for kc in range(KC):
                        S.add("pe", lambda e, kc=kc, pvb=pvb, half=half: e.matmul(
                            bank(pvb), lhsT=hT[:, kc, tok], rhs=Wv_l[half][:, kc, :], start=(kc == 0), stop=(kc == KC - 1)),
                            r=[("wb", wv_i[half]), ("hT", t)], w=[("ps", pvb)])
                    S.add("dve", lambda e, half=half, pvb=pvb: e.tensor_copy(out=vt[:, half * 512:(half + 1) * 512], in_=bank(pvb)),
                          r=[("ps", pvb)], w=[("v", par, half), ("ps", pvb)])

            def st_gproj(c):
                t, par, tok, gs = c["t"], c["par"], c["tok"], c["gs"]
                Wg_l = list(Wg)
                for half in range(2):
                    pgb = nb4()
                    hs = slice(half * 512, (half + 1) * 512)
                    for kc in range(KC):
                        S.add("pe", lambda e, kc=kc, pgb=pgb, half=half: e.matmul(
                            bank(pgb), lhsT=hT[:, kc, tok], rhs=Wg_l[half][:, kc, :], start=(kc == 0), stop=(kc == KC - 1)),
                            r=[("wb", wgg_i[half]), ("hT", t)], w=[("ps", pgb)])
                    S.add("act", lambda e, hs=hs, pgb=pgb: e.activation(out=gs[:, hs], in_=bank(pgb), func=AF.Tanh, scale=0.5),
                          r=[("ps", pgb)], w=[("gs", par, half), ("ps", pgb)])
                    S.add("dve", lambda e, hs=hs, pgb=pgb: e.scalar_tensor_tensor(
                        out=gs[:, hs], in0=gs[:, hs], scalar=1.0, in1=bank(pgb), op0=ALU.add, op1=ALU.mult),
                        r=[("ps", pgb), ("gs", par, half)], w=[("gs", par, half), ("ps", pgb)])
                    S.add("pool", lambda e, hs=hs: e.tensor_tensor(out=gs[:, hs], in0=gs[:, hs], in1=gnb[:, hs], op=ALU.mult),
                          r=[("gs", par, half), "gnb"], w=[("gs", par, half)])

            def st_upd(c, ci):
                t, par, kdec, vt = c["t"], c["par"], c["kdec"], c["vt"]
                pus = [nb4(), nb4()]
                for h in range(4):
                    pbk = pus[h // 2]
                    S.add("pe", lambda e, h=h, pbk=pbk: e.matmul(
                        bank(pbk, 256, (h % 2) * 256), lhsT=kdec[64 * ci:64 * ci + 64, h * 128:(h + 1) * 128],
                        rhs=vt[64 * ci:64 * ci + 64, h * 256:(h + 1) * 256], start=True, stop=True),
                        r=[("kdec", par), ("v", par, h // 2)], w=[("ps", pbk)])
                for h in range(4):
                    pbk = pus[h // 2]
                    S.add("dve", lambda e, h=h, pbk=pbk: e.scalar_tensor_tensor(
                        out=Sst[:, h, :], in0=Sst[:, h, :], scalar=decT[:, t, h, ci:ci + 1],
                        in1=bank(pbk, 256, (h % 2) * 256), op0=ALU.mult, op1=ALU.add),
                        r=[("ps", pbk), ("decT", t), ("Sst", h)], w=[("Sst", h), ("ps", pbk)])
                S.add("act", lambda e: e.copy(out=Sb[ci], in_=Sst), r=[("Sst", h) for h in range(4)], w=[("Sb", ci)])

            def st_o(c, ci):
                t, tb, po2 = c["t"], c["tb"], c["po2"]
                ch = 2 * t + ci
                for h in range(4):
                    S.add("pe", lambda e, h=h: e.matmul(
                        ps[64 * ci:64 * ci + 64, (po2 + h // 2) * 512 + (h % 2) * 256:(po2 + h // 2) * 512 + (h % 2) * 256 + 256],
                        lhsT=qgT[:, h, ch * 64:(ch + 1) * 64], rhs=Sb[ci][:, h, :], start=True, stop=True),
                        r=[("Sb", ci), ("qgT", tb)], w=[("ps", po2 + h // 2)])

            def st_norm(c):
                t, par, po2, gs, zt = c["t"], c["par"], c["po2"], c["gs"], c["zt"]
                cst = stat_col()
                ssq = stat[:, cst:cst + 4]
                kst = ("stat", cst)
                osrc = ps[:, po2 * 512:(po2 + 2) * 512].rearrange("p (h d) -> p h d", h=4)
                for h in range(4):
                    pk = ("ps", po2 + h // 2)
                    S.add("act", lambda e, h=h: e.activation(
                        out=zt[:, h * 256:(h + 1) * 256], in_=osrc[:, h, :], func=AF.Square, accum_out=ssq[:, h:h + 1]),
                        r=[pk], w=[kst, ("z", par), pk])
                S.add("dve", lambda e: e.tensor_scalar(out=ssq, in0=ssq, scalar1=1.0 / 256.0, scalar2=EPS,
                                                       op0=ALU.mult, op1=ALU.add), r=[kst], w=[kst])
                S.add("pool", lambda e: e.tensor_tensor(out=ssq, in0=ssq, in1=neghalf[:, 0:4], op=ALU.pow),
                      r=[kst, "neghalf"], w=[kst])

                def zmul():
                    for h in range(4):
                        pk = ("ps", po2 + h // 2)
                        S.add("dve", lambda e, h=h: e.scalar_tensor_tensor(
                            out=zt[:, h * 256:(h + 1) * 256], in0=osrc[:, h, :], scalar=ssq[:, h:h + 1],
                            in1=gs[:, h * 256:(h + 1) * 256], op0=ALU.mult, op1=ALU.mult),
                            r=[pk, kst, ("gs", par, h // 2)], w=[("z", par), pk])
                return zmul

            def st_tr(c):
                t, par, tok, zt = c["t"], c["par"], c["tok"], c["zt"]
                for half in range(2):
                    ptr = nb4()
                    for j in range(4):
                        kc = half * 4 + j
                        S.add("pe", lambda e, kc=kc, j=j, ptr=ptr: e.matmul(
                            bank(ptr, 128, j * 128), lhsT=zt[:, kc * 128:(kc + 1) * 128], rhs=ident[:],
                            start=True, stop=True),
                            r=[("z", par), "ident"], w=[("ps", ptr)])
                    src = bank(ptr).rearrange("p (a b) -> p a b", a=4)
                    dst = zgT[:, half * 4:(half + 1) * 4, tok]
                    if half == 0:
                        S.add("act", lambda e, src=src, dst=dst: e.copy(out=dst, in_=src),
                              r=[("ps", ptr)], w=[("zgT", t), ("ps", ptr)])
                    else:
                        S.add("dve", lambda e, src=src, dst=dst: e.tensor_copy(out=dst, in_=src),
                              r=[("ps", ptr)], w=[("zgT", t), ("ps", ptr)])

            for it in range(NTT + 3):
                c0 = tile_ctx(it) if it < NTT else None
                c1 = tile_ctx(it - 1) if 0 <= it - 1 < NTT else None
                c2 = tile_ctx(it - 2) if 0 <= it - 2 < NTT else None
                c3 = tile_ctx(it - 3) if 0 <= it - 3 < NTT else None
                zmul = st_norm(c3) if c3 else None
                if c0 and it % 2 == 0:
                    st_gate2(it)
                if c2:
                    st_upd(c2, 0)
                if zmul:
                    zmul()
                if c2:
                    st_upd(c2, 1)
                if c1:
                    st_rev(c1)
                    st_kproj(c1)
                    st_vproj(c1)
                if c2:
                    st_o(c2, 0)
                    st_o(c2, 1)
                    st_gproj(c2)
                if c3:
                    st_tr(c3)

            for dpos in range(5):
                wrelease(wpos_k + dpos)

            if "zgT" in dbg_d and seq == 0:
                S.barrier()
                D2 = Arena()
                D2.off = 56 * 1024
                tmp = D2.take([2048], F32)
                for kc in range(KC):
                    S.add("dve", lambda e, kc=kc: e.tensor_copy(out=tmp, in_=zgT[:, kc, :]),
                          r=[("zgT", t) for t in range(NTT)], w=["dbgtmp"])
                    S.add("sp", lambda e, kc=kc: e.dma_start(out=dbg_d["zgT"][:, kc * T:(kc + 1) * T], in_=tmp),
                          r=["dbgtmp"], dma="dbg")


        if "D" in phases:
            S.barrier()
            Dn = Arena()
            mT = Dn.take([KC, T], BF16)
            tgR = [Dn.take([2, 512], F32) for _ in range(2)]
            t1R = [Dn.take([512], F32) for _ in range(1)]
            xtD = [Dn.take([1024], F32) for _ in range(2)]
            yoD = [Dn.take([1024], F32) for _ in range(2)]
            gpost = Dn.take([1024], F32)
            sqD = [Dn.take([1024], BF16) for _ in range(2)]
            rr = {"n": 0}

            def nb(k=1):
                if k == 2 and rr["n"] % 2:
                    rr["n"] += 1
                b_ = rr["n"] % 8
                rr["n"] += k
                return b_

            hT_all = [("hT", t) for t in range(NTT)]
            zg_all = [("zgT", t) for t in range(NTT)]
            S.add("sp", lambda e: e.dma_start(out=gpost, in_=vecs_d[1:2, :].partition_broadcast(128)),
                  w=["gpost"], dma="c_gpost")
            it = 0
            for dc in range(8):
                wi, wpos = wnext(G_D0 + dc)
                W = wb[wi]
                kw = ("wb", wi)
                for tb in range(NTB):
                    tsl = slice(tb * 512, (tb + 1) * 512)
                    tg = tgR[it % 2]
                    t1 = t1R[0]
                    ktg = ("tgD", it % 2)
                    it += 1
                    pbs = []
                    for j, (src, rkeys) in enumerate(((hT, hT_all[tb * 4:tb * 4 + 4]), (hT, hT_all[tb * 4:tb * 4 + 4]),
                                                     (zgT, zg_all[tb * 4:tb * 4 + 4]), (zaT, [("zaT", tb // 2)]))):
                        pb = nb()
                        pbs.append(pb)
                        for kc in range(KC):
                            S.add("pe", lambda e, pb=pb, kc=kc, j=j, src=src, W=W, tsl=tsl: e.matmul(
                                bank(pb), lhsT=W[:, kc, j * 128:(j + 1) * 128], rhs=src[:, kc, tsl],
                                start=(kc == 0), stop=(kc == KC - 1)),
                                r=[kw] + rkeys, w=[("ps", pb)])
                        if j < 2:
                            col = dc if j == 0 else 8 + dc
                            S.add("act", lambda e, pb=pb, j=j, tg=tg, col=col: e.activation(
                                out=tg[:, j, :], in_=bank(pb), func=AF.Tanh, bias=mbias[:, col:col + 1], scale=0.5),
                                r=[("ps", pb), "mbias"], w=[ktg, ("ps", pb)])
                    S.add("dve", lambda e, tg=tg, t1=t1, pb=pbs[2]: e.scalar_tensor_tensor(
                        out=t1, in0=tg[:, 0, :], scalar=1.0, in1=bank(pb), op0=ALU.add, op1=ALU.mult),
                        r=[ktg, ("ps", pbs[2])], w=["t1", ("ps", pbs[2])])
                    S.add("dve", lambda e, tg=tg, pb=pbs[3]: e.scalar_tensor_tensor(
                        out=tg[:, 1, :], in0=tg[:, 1, :], scalar=1.0, in1=bank(pb), op0=ALU.add, op1=ALU.mult),
                        r=[ktg, ("ps", pbs[3])], w=[ktg, ("ps", pbs[3])])
                    S.add("pool", lambda e, tg=tg, t1=t1, dc=dc, tsl=tsl: e.tensor_tensor(
                        out=mT[:, dc, tsl], in0=t1, in1=tg[:, 1, :], op=ALU.add),
                        r=["t1", ktg], w=[("mT", tb)])
                wrelease(wpos)
            wo0, wpos_o = wnext(G_O0)
            wo_i = [wo0, wnext(G_O0 + 1)[0]]
            Wo_l = [wb[wo_i[0]], wb[wo_i[1]]]

            def d_mm(tt, seq=seq):
                par = tt % 2
                tok = slice(tt * 128, (tt + 1) * 128)
                xi, sq = xtD[par], sqD[par]
                kx = ("xtD", par)
                S.add("sp", lambda e: e.dma_start(out=xi, in_=x_d[seq, tok, :]), w=[kx], dma="xtD%d" % par)
                py = nb(2)
                for half in range(2):
                    for kc in range(KC):
                        S.add("pe", lambda e, half=half, kc=kc: e.matmul(
                            bank(py + half), lhsT=mT[:, kc, tok], rhs=Wo_l[half][:, kc, :],
                            start=(kc == 0), stop=(kc == KC - 1)),
                            r=[("wb", wo_i[half]), ("mT", tt // 4)], w=[("ps", py + half)])
                ysrc = ps[:, py * 512:(py + 2) * 512]
                pyk = [("ps", py), ("ps", py + 1)]
                cst = stat_col()
                ssq = stat[:, cst:cst + 1]
                kst = ("stat", cst)
                S.add("act", lambda e: e.activation(out=sq, in_=ysrc, func=AF.Square, accum_out=ssq),
                      r=pyk, w=[kst, ("sqD", par)] + pyk)
                S.add("dve", lambda e: e.tensor_scalar(out=ssq, in0=ssq, scalar1=1.0 / D, scalar2=4.0 * EPS,
                                                       op0=ALU.mult, op1=ALU.add), r=[kst], w=[kst])
                S.add("pool", lambda e: e.tensor_tensor(out=ssq, in0=ssq, in1=neghalf[:, 0:1], op=ALU.pow),
                      r=[kst, "neghalf"], w=[kst])
                return ysrc, pyk, ssq, kst

            def d_fin(tt, ysrc, pyk, ssq, kst, seq=seq):
                par = tt % 2
                tok = slice(tt * 128, (tt + 1) * 128)
                xi, yo = xtD[par], yoD[par]
                kx = ("xtD", par)
                S.add("dve", lambda e: e.scalar_tensor_tensor(
                    out=yo, in0=ysrc, scalar=ssq, in1=gpost, op0=ALU.mult, op1=ALU.mult),
                    r=pyk + [kst, "gpost"], w=[("yoD", par)] + pyk)
                S.add("pool" if tt % 2 else "dve", lambda e: e.tensor_tensor(out=xi, in0=xi, in1=yo, op=ALU.add),
                      r=[kx, ("yoD", par)], w=[kx])
                S.add("sp", lambda e: e.dma_start(out=out_d[seq, tok, :], in_=xi), r=[kx], dma="oD%d" % par)

            d_st = {}
            for it in range(NTT + 1):
                if it < NTT:
                    d_st[it] = d_mm(it)
                if it >= 1:
                    d_fin(it - 1, *d_st[it - 1])
            wrelease(wpos_o)
            wrelease(wpos_o + 1)

    w_try_issue(limit=2)
    for seq_ in range(nseq):
        do_seq(seq_)

    S.finalize()
    sem_names = list(S.ENGS[:4]) + S.dma_sems
    import contextlib
    with contextlib.ExitStack() as es:
        sems = {n: es.enter_context(nc.semaphore("s_" + str(n))) for n in sem_names}
        with nc.Block() as block:
            @block.tensor
            def _(e):
                S.emit("pe", e, sems)

            @block.scalar
            def _(e):
                S.emit("act", e, sems)

            @block.vector
            def _(e):
                S.emit("dve", e, sems)

            @block.gpsimd
            def _(e):
                S.emit("pool", e, sems)

            @block.sync
            def _(e):
                S.emit("sp", e, sems)
                for n in S.dma_sems:
                    e.wait_ge(sems[n], S.final_counts[n])
    return nc


def _prep_shared(norm_pre_g, w_in, gk_up, gk_bias, gla_norm_g, rel_bias, w_o_gla, w_o_att,
                 merge_bias, w_out, norm_post_g):
    w_in = np.asarray(w_in, np.float32)
    o_qg, o_kg, o_vg, o_gg, o_code = 0, 512, 1024, 2048, 3072
    o_qa, o_ka, o_va, o_ga, o_gate = 3088, 4112, 5136, 6160, 7184
    groups = []
    for hp in range(8):
        cols = np.concatenate([np.arange(o + hp * 128, o + hp * 128 + 128) for o in (o_qa, o_ka, o_va, o_ga)])
        groups.append(w_in[:, cols])
    groups.append(w_in[:, o_qg:o_qg + 512])
    groups.append(w_in[:, o_kg:o_kg + 512])
    groups.append(w_in[:, o_vg:o_vg + 512])
    groups.append(w_in[:, o_vg + 512:o_vg + 1024])
    groups.append(w_in[:, o_gg:o_gg + 512])
    groups.append(w_in[:, o_gg + 512:o_gg + 1024])
    w_o_gla = np.asarray(w_o_gla, np.float32)
    w_o_att = np.asarray(w_o_att, np.float32)
    w_out = np.asarray(w_out, np.float32)
    for dc in range(8):
        sl = slice(dc * 128, dc * 128 + 128)
        groups.append(np.concatenate([w_in[:, o_gate + dc * 128:o_gate + dc * 128 + 128],
                                      w_in[:, o_gate + 1024 + dc * 128:o_gate + 1024 + dc * 128 + 128],
                                      w_o_gla[:, sl], w_o_att[:, sl]], axis=1))
    groups.append(w_out[:, 0:512])
    groups.append(w_out[:, 512:1024])
    wg = np.stack(groups, 0)
    wg = wg.reshape(NGROUPS, KC, 128, 512).transpose(0, 2, 1, 3).reshape(NGROUPS, 128, KC * 512)
    wg = np.ascontiguousarray(wg)
    wcode = w_in[:, o_code:o_code + 16].reshape(KC, 128, 16).transpose(1, 0, 2).reshape(128, KC * 16)
    gkup = np.concatenate([np.asarray(gk_up, np.float32), np.asarray(gk_bias, np.float32)[None, :]], 0)
    vecs = np.stack([np.asarray(norm_pre_g, np.float32), np.asarray(norm_post_g, np.float32),
                     np.tile(np.asarray(gla_norm_g, np.float32), 4)], 0)
    mb = np.asarray(merge_bias, np.float32).reshape(16, 128).T
    kk = np.arange(128)[:, None]
    qq = np.arange(640)[None, :]
    relb = np.asarray(rel_bias, np.float32)[:, np.clip(qq - kk, -256, 256) + 256]
    ident = np.eye(128, dtype=np.float32)
    tp = np.arange(128)
    revtri = ((tp[:, None] > tp[None, :]) & (tp[:, None] // 64 == tp[None, :] // 64)).astype(np.float32)
    cind = (tp[:, None] // 64 == np.arange(2)[None, :]).astype(np.float32)
    return {
        "wg": wg, "wcode": np.ascontiguousarray(wcode), "gkup": np.ascontiguousarray(gkup),
        "vecs": np.ascontiguousarray(vecs), "mbias": np.ascontiguousarray(mb),
        "relb": np.ascontiguousarray(relb), "ident": ident, "revtri": revtri, "cind": cind,
    }


_NC_CACHE = {}


def kernel(x, norm_pre_g, w_in, gk_up, gk_bias, gla_norm_g, rel_bias, w_o_gla, w_o_att,
           merge_bias, w_out, norm_post_g, _debug=None, _nseq=NSEQ, _phases="ABCD"):
    x = np.asarray(x, np.float32)
    shared = _prep_shared(norm_pre_g, w_in, gk_up, gk_bias, gla_norm_g, rel_bias, w_o_gla, w_o_att,
                          merge_bias, w_out, norm_post_g)
    nc = build_nc(debug=_debug, nseq=_nseq, phases=_phases)
    in_maps = []
    for c in range(N_CORES):
        m = dict(shared)
        m["x"] = np.ascontiguousarray(x[2 * c:2 * c + 2])
        in_maps.append(m)
    res = run_bass_kernel_spmd(nc, in_maps, core_ids=list(range(N_CORES)))
    out = np.concatenate([np.asarray(r["out"]) for r in res.results], axis=0).astype(np.float32)
    if _debug is not None:
        return out, res.results
    return out
```

```python
import os
import numpy as np
import ml_dtypes
import concourse.bass as bass
import concourse.mybir as mybir
from concourse.bass_utils import run_bass_kernel_spmd

F32 = mybir.dt.float32
BF16 = mybir.dt.bfloat16
U8 = mybir.dt.uint8
AF = mybir.ActivationFunctionType
ALU = mybir.AluOpType
AX = mybir.AxisListType

N_CORES = 8
D = 1024
T = 2048
NSEQ = 2
KC = 8
NTT = T // 128
NTB = T // 512
EPS = 1e-6
NGROUPS = 24
G_B0, G_Q, G_K, G_V0, G_V1, G_G0, G_G1, G_D0, G_O0 = 0, 8, 9, 10, 11, 12, 13, 14, 22
NEG = -30000.0


class Op:
    __slots__ = ("eng", "fn", "deps", "idx", "signal", "count", "sem", "is_dma", "waits", "done")

    def __init__(self, eng, fn, is_dma, sem):
        self.eng = eng
        self.fn = fn
        self.deps = set()
        self.signal = False
        self.count = 0
        self.sem = sem
        self.is_dma = is_dma
        self.waits = []
        self.done = None


class Sched:
    ENGS = ("pe", "act", "dve", "pool", "sp")

    def __init__(self):
        self.ops = []
        self.per_eng = {e: [] for e in self.ENGS}
        self.last_w = {}
        self.readers = {}
        self.dma_sems = []
        self.last_by_sem = {}
        self.bar_deps = []
        self.bar_seen = set()

    def barrier(self):
        self.bar_deps = [o for s, o in self.last_by_sem.items() if not (isinstance(s, str) and s.startswith("w"))]
        self.bar_seen = set()

    def add(self, eng, fn, r=(), w=(), dma=None):
        is_dma = dma is not None
        op = Op(eng, fn, is_dma, dma if is_dma else eng)
        if is_dma and dma not in self.dma_sems:
            self.dma_sems.append(dma)
        op.idx = len(self.ops)
        deps = op.deps
        if self.bar_deps and eng != "pe" and eng not in self.bar_seen:
            self.bar_seen.add(eng)
            for o in self.bar_deps:
                deps.add(o)
        self.last_by_sem[op.sem] = op
        for k in r:
            lw = self.last_w.get(k)
            if lw is not None:
                deps.add(lw)
        for k in w:
            lw = self.last_w.get(k)
            if lw is not None and (lw.eng != eng or lw.is_dma or is_dma or eng != 'pe'):
                deps.add(lw)
            rd = self.readers.get(k)
            if rd:
                for o in rd.values():
                    if o.eng != eng or o.is_dma or is_dma or eng != 'pe':
                        deps.add(o)
        for k in r:
            self.readers.setdefault(k, {})[(eng, op.idx) if is_dma else eng] = op
        for k in w:
            self.last_w[k] = op
            self.readers[k] = {}
        self.ops.append(op)
        self.per_eng[eng].append(op)
        return op

    def finalize(self):
        for op in self.ops:
            for d in op.deps:
                d.signal = True
        cnt = {}
        for op in self.ops:
            if op.is_dma:
                cnt[op.sem] = cnt.get(op.sem, 0) + 16
                op.count = cnt[op.sem]
            elif op.signal:
                cnt[op.sem] = cnt.get(op.sem, 0) + 1
                op.count = cnt[op.sem]
        self.final_counts = cnt
        clocks = {e: {} for e in self.ENGS}
        for op in self.ops:
            clk = clocks[op.eng]
            waits = {}
            for d in sorted(op.deps, key=lambda o: o.idx):
                if clk.get(d.sem, 0) < d.count:
                    waits[d.sem] = max(waits.get(d.sem, 0), d.count)
                    for s, v in d.done.items():
                        if clk.get(s, 0) < v:
                            clk[s] = v
            op.waits = list(waits.items())
            done = dict(clk)
            if op.is_dma or op.signal:
                if done.get(op.sem, 0) < op.count:
                    done[op.sem] = op.count
            op.done = done

    def emit(self, name, eng, sems):
        for op in self.per_eng[name]:
            for s, v in op.waits:
                eng.wait_ge(sems[s], v)
            ins = op.fn(eng)
            if op.is_dma:
                ins.then_inc(sems[op.sem], 16)
            elif op.signal:
                ins.then_inc(sems[op.sem], 1)


def build_nc(debug=None, nseq=NSEQ, phases="ABCD"):
    debug = debug or {}
    nc = bass.Bass("TRN2", target_bir_lowering=False)
    S = Sched()

    x_d = nc.dram_tensor("x", [NSEQ, T, D], F32, kind="ExternalInput")
    wg_d = nc.dram_tensor("wg", [NGROUPS, 128, KC * 512], F32, kind="ExternalInput")
    wcode_d = nc.dram_tensor("wcode", [128, KC * 16], F32, kind="ExternalInput")
    gkup_d = nc.dram_tensor("gkup", [17, 512], F32, kind="ExternalInput")
    vecs_d = nc.dram_tensor("vecs", [3, 1024], F32, kind="ExternalInput")
    mbias_d = nc.dram_tensor("mbias", [128, 16], F32, kind="ExternalInput")
    relb_d = nc.dram_tensor("relb", [16, 128, 640], F32, kind="ExternalInput")
    ident_d = nc.dram_tensor("ident", [128, 128], F32, kind="ExternalInput")
    revtri_d = nc.dram_tensor("revtri", [128, 128], F32, kind="ExternalInput")
    cind_d = nc.dram_tensor("cind", [128, 2], F32, kind="ExternalInput")
    out_d = nc.dram_tensor("out", [NSEQ, T, D], F32, kind="ExternalOutput")
    dbg_d = {}
    for name, shape in debug.items():
        dbg_d[name] = nc.dram_tensor("dbg_" + name, list(shape), F32, kind="ExternalOutput")

    hT = nc.alloc_sbuf_tensor("hT", [128, KC, T], BF16)
    zaT = nc.alloc_sbuf_tensor("zaT", [128, KC, T], BF16)
    zgT = nc.alloc_sbuf_tensor("zgT", [128, KC, T], BF16)
    NWB = 5
    wb = [nc.alloc_sbuf_tensor("wb%d" % i, [128, KC, 512], BF16) for i in range(NWB)]
    ident = nc.alloc_sbuf_tensor("ident_bf", [128, 128], BF16)
    identf = nc.alloc_sbuf_tensor("ident_f", [128, 128], F32)
    revtri = nc.alloc_sbuf_tensor("revtri_sb", [128, 128], F32)
    cind = nc.alloc_sbuf_tensor("cind_sb", [128, 2], F32)
    mbias = nc.alloc_sbuf_tensor("mbias_sb", [128, 16], F32)
    gkup = nc.alloc_sbuf_tensor("gkup_sb", [17, 512], F32)
    wcode = nc.alloc_sbuf_tensor("wcode_sb", [128, KC, 16], BF16)
    neghalf = nc.alloc_sbuf_tensor("neghalf", [128, 8], F32)
    revtri_b = nc.alloc_sbuf_tensor("revtri_b", [128, 128], BF16)
    cind_b = nc.alloc_sbuf_tensor("cind_b", [128, 2], BF16)
    gkup_b = nc.alloc_sbuf_tensor("gkup_b", [17, 512], BF16)
    stat = nc.alloc_sbuf_tensor("stat", [128, 64], F32)
    ARENA_BYTES = 68288
    arena = nc.alloc_sbuf_tensor("arena", [128, ARENA_BYTES], U8)
    ps = nc.alloc_psum_tensor("ps", [128, 4096], F32)
    ps_bf = ps.bitcast(BF16)

    class Arena:
        def __init__(self):
            self.off = 0

        def take(self, free_shape, dtype):
            n = int(np.prod(free_shape)) * mybir.dt.size(dtype)
            n_al = (n + 31) // 32 * 32
            assert self.off + n_al <= ARENA_BYTES, ("arena overflow", self.off, n_al)
            ap = arena[:, self.off:self.off + n].bitcast(dtype)
            self.off += n_al
            if len(free_shape) == 2:
                ap = ap.rearrange("p (a b) -> p a b", a=free_shape[0])
            elif len(free_shape) == 3:
                ap = ap.rearrange("p (a b c) -> p a b c", a=free_shape[0], b=free_shape[1])
            elif len(free_shape) == 4:
                ap = ap.rearrange("p (a b c d) -> p a b c d", a=free_shape[0], b=free_shape[1], c=free_shape[2])
            return ap

    def bank(b, n=512, off=0):
        return ps[:, b * 512 + off: b * 512 + off + n]

    def bank_bf(b, n=1024, off=0):
        return ps_bf[:, b * 1024 + off: b * 1024 + off + n]

    def cload(dst_ap, src_ap, key, tmp=None):
        S.add("sp", lambda e, d=dst_ap, s=src_ap: e.dma_start(out=d, in_=s), w=[key], dma="c_" + str(key))

    cload(identf[:], ident_d.ap(), "identf")
    cload(revtri[:], revtri_d.ap(), "revtri")
    cload(cind[:], cind_d.ap(), "cind")
    cload(mbias[:], mbias_d.ap(), "mbias")
    cload(gkup[:], gkup_d.ap(), "gkup")
    S.add("dve", lambda e: e.tensor_copy(out=ident[:], in_=identf[:]), r=["identf"], w=["ident"])
    S.add("pool", lambda e: e.memset(neghalf[:], -0.5), w=["neghalf"])
    S.add("dve", lambda e: e.tensor_copy(out=revtri_b[:], in_=revtri[:]), r=["revtri"], w=["revtri_b"])
    S.add("dve", lambda e: e.tensor_copy(out=cind_b[:], in_=cind[:]), r=["cind"], w=["cind_b"])
    S.add("dve", lambda e: e.tensor_copy(out=gkup_b[:], in_=gkup[:]), r=["gkup"], w=["gkup_b"])
    S.add("dve", lambda e: e.tensor_scalar(out=mbias[:], in0=mbias[:], scalar1=0.5, scalar2=None, op0=ALU.mult),
          r=["mbias"], w=["mbias"])
    S.add("pool", lambda e: e.dma_start(out=wcode[:].rearrange("p a b -> p (a b)"), in_=wcode_d.ap()),
          w=["wcode"], dma="c_wcode")

    per_seq = []
    if "B" in phases:
        per_seq += [G_B0 + i for i in range(8)]
    if "C" in phases:
        per_seq += [G_Q, G_K, G_V0, G_V1, G_G0, G_G1]
    if "D" in phases:
        per_seq += [G_D0 + i for i in range(8)] + [G_O0, G_O0 + 1]
    wsched = per_seq * nseq
    wstate = {"issued": 0, "pos": 0}
    released = set()

    def w_try_issue(limit=None):
        while wstate["issued"] < (len(wsched) if limit is None else min(limit, len(wsched))) and (wstate["issued"] < NWB or (wstate["issued"] - NWB) in released):
            j = wstate["issued"]
            i = j % NWB
            g = wsched[j]
            S.add("pool", lambda e, i=i, g=g: e.dma_start(out=wb[i][:].rearrange("p a b -> p (a b)"), in_=wg_d[g]),
                  w=[("wb", i)], dma="w%d" % i)
            wstate["issued"] += 1

    def wnext(expect_g):
        pos = wstate["pos"]
        assert wsched[pos] == expect_g, (pos, wsched[pos], expect_g)
        w_try_issue()
        assert wstate["issued"] > pos, ("weight slot not released in time", pos)
        wstate["pos"] += 1
        return pos % NWB, pos

    def wrelease(pos):
        released.add(pos)
        w_try_issue()

    stat_n = {"n": 0}

    def stat_col(n=1):
        c = stat_n["n"] % (64 // 4) * 4
        stat_n["n"] += 1
        return c

    def do_seq(seq):
        S.barrier()
        A = Arena()
        gpre = A.take([1024], F32)
        NXT = 12
        xt = [A.take([1024], F32) for _ in range(NXT)]
        sqr = [A.take([1024], BF16) for _ in range(2)]
        NHB = 4
        hb = [A.take([1024], BF16) for _ in range(NHB)]
        S.add("sp", lambda e, d=gpre: e.dma_start(out=d, in_=vecs_d[0:1, :].partition_broadcast(128)),
              w=["gpre"], dma="c_gpre")
        def a_load(tt, seq=seq):
            xi = xt[tt % NXT]
            kx = ("xt", tt % NXT)
            S.add("sp", lambda e: e.dma_start(out=xi, in_=x_d[seq, tt * 128:(tt + 1) * 128, :]),
                  w=[kx], dma="xt%d" % (tt % NXT))
            c = stat_col()
            ssq = stat[:, c:c + 1]
            ks = ("stat", c)
            S.add("act", lambda e, sq=sqr[tt % 2]: e.activation(
                out=sq, in_=xi, func=AF.Square, accum_out=ssq), r=[kx], w=[ks, ("sq", tt % 2)])
            S.add("dve", lambda e: e.tensor_scalar(out=ssq, in0=ssq, scalar1=1.0 / D, scalar2=EPS,
                                                   op0=ALU.mult, op1=ALU.add), r=[ks], w=[ks])
            S.add("pool", lambda e: e.tensor_tensor(out=ssq, in0=ssq, in1=neghalf[:, 0:1], op=ALU.pow),
                  r=[ks, "neghalf"], w=[ks])
            return ssq, ks

        def a_norm(tt, ssq, ks):
            xi = xt[tt % NXT]
            hi = hb[tt % NHB]
            kx = ("xt", tt % NXT)
            kh = ("hb", tt % NHB)
            S.add("dve", lambda e: e.scalar_tensor_tensor(
                out=hi, in0=xi, scalar=ssq, in1=gpre, op0=ALU.mult, op1=ALU.mult),
                r=[kx, ks, "gpre"], w=[kh])
            pb = 2 * (tt % 4)
            for kc in range(KC):
                S.add("pe", lambda e, kc=kc: e.matmul(
                    bank(pb + kc // 4, 128, (kc % 4) * 128), lhsT=hi[:, kc * 128:(kc + 1) * 128], rhs=ident[:],
                    start=True, stop=True),
                    r=[kh, "ident"], w=[("ps", pb + kc // 4)])

        def a_copy(tt):
            pb = 2 * (tt % 4)
            kp = [("ps", pb), ("ps", pb + 1)]
            src = ps[:, pb * 512:(pb + 2) * 512].rearrange("p (a b) -> p a b", a=KC)
            dst = hT[:, :, tt * 128:(tt + 1) * 128]
            if tt % 2 == 0:
                S.add("act", lambda e: e.copy(out=dst, in_=src), r=kp, w=[("hT", tt)] + kp)
            else:
                S.add("dve", lambda e: e.tensor_copy(out=dst, in_=src), r=kp, w=[("hT", tt)] + kp)

        a_stats = {}
        for it in range(NTT + 2):
            if it < NTT:
                a_stats[it] = a_load(it)
            if 0 <= it - 1 < NTT:
                a_norm(it - 1, *a_stats[it - 1])
            if 0 <= it - 2 < NTT:
                a_copy(it - 2)

        if "hT" in dbg_d and seq == 0:
            S.barrier()
            D2 = Arena()
            D2.off = 32 * 1024
            tmp = D2.take([2048], F32)
            for kc in range(KC):
                S.add("dve", lambda e, kc=kc: e.tensor_copy(out=tmp, in_=hT[:, kc, :]),
                      r=[("hT", t) for t in range(NTT)], w=["dbgtmp"])
                S.add("sp", lambda e, kc=kc: e.dma_start(out=dbg_d["hT"][:, kc * T:(kc + 1) * T], in_=tmp),
                      r=["dbgtmp"], dma="dbg")


        if "B" in phases:
            S.barrier()
            B = Arena()
            qTb, kTb, Vaugb, gsAb = [], [], [], []
            for _p in range(2):
                qTb.append(B.take([T], BF16))
                kTb.append(B.take([T], BF16))
                Vaugb.append(B.take([NTT, 2, 65], BF16))
                gsAb.append(B.take([NTT, 128], BF16))
            tgA = [B.take([2, 128], F32) for _ in range(1)]
            eB = B.take([2, 640], F32)
            eBb = B.take([2, 640], BF16)
            sbp = [B.take([2, 640], BF16) for _ in range(2)]
            PT = [B.take([2, 640], BF16) for _ in range(6)]
            zall = B.take([NTT, 128], BF16)
            for p in range(2):
                S.add("pool", lambda e, p=p: e.memset(Vaugb[p][:, :, :, 64:65], 2.0), w=[("vaug_ones", p)])
            proj_ring = {"n": 0}
            smain_ring = {"n": 0}
            hT_all = [("hT", t) for t in range(NTT)]

            def proj_items(hp):
                p = hp % 2
                wi, wpos = wnext(G_B0 + hp)
                W = wb[wi]
                kw = ("wb", wi)
                qT, kT, Vaug, gsA = qTb[p], kTb[p], Vaugb[p], gsAb[p]
                items = []

                def fm(which, tb):
                    dstT = qT if which == 0 else kT
                    kd = ("qT", p, tb) if which == 0 else ("kT", p, tb)
                    pb = proj_ring["n"] % 2
                    proj_ring["n"] += 1
                    for kc in range(KC):
                        S.add("pe", lambda e, pb=pb, kc=kc: e.matmul(
                            bank(pb), lhsT=W[:, kc, which * 128:(which + 1) * 128],
                            rhs=hT[:, kc, tb * 512:(tb + 1) * 512], start=(kc == 0), stop=(kc == KC - 1)),
                            r=[kw] + hT_all[tb * 4:tb * 4 + 4], w=[("ps", pb)])
                    if which == 0:
                        S.add("act", lambda e, pb=pb: e.activation(
                            out=dstT[:, tb * 512:(tb + 1) * 512], in_=bank(pb), func=AF.Copy, scale=0.125),
                            r=[("ps", pb)], w=[kd, ("ps", pb)])
                    else:
                        S.add("dve", lambda e, pb=pb: e.tensor_copy(
                            out=dstT[:, tb * 512:(tb + 1) * 512], in_=bank(pb)),
                            r=[("ps", pb)], w=[kd, ("ps", pb)])

                def tm(tp):
                    pb = proj_ring["n"] % 2
                    proj_ring["n"] += 1
                    for j in range(2):
                        tt = 2 * tp + j
                        for kc in range(KC):
                            S.add("pe", lambda e, pb=pb, kc=kc, tt=tt, j=j: e.matmul(
                                bank(pb, 256, j * 256), lhsT=hT[:, kc, tt * 128:(tt + 1) * 128],
                                rhs=W[:, kc, 256:512], start=(kc == 0), stop=(kc == KC - 1)),
                                r=[kw, ("hT", tt)], w=[("ps", pb)])
                    pview = bank(pb).rearrange("p (a b) -> p a b", a=2)
                    vsrc = pview[:, :, 0:128].rearrange("p a (h d) -> p a h d", h=2)
                    S.add("dve", lambda e: e.tensor_copy(out=Vaug[:, 2 * tp:2 * tp + 2, :, 0:64], in_=vsrc),
                          r=[("ps", pb)], w=[("V", p, tp), ("ps", pb)])
                    tg = tgA[0]
                    S.add("act", lambda e: e.activation(out=tg, in_=pview[:, :, 128:256], func=AF.Tanh, scale=0.5),
                          r=[("ps", pb)], w=[("tgA", 0), ("ps", pb)])
                    S.add("dve", lambda e: e.scalar_tensor_tensor(
                        out=gsA[:, 2 * tp:2 * tp + 2, :], in0=tg, scalar=1.0, in1=pview[:, :, 128:256],
                        op0=ALU.add, op1=ALU.mult),
                        r=[("ps", pb), ("tgA", 0)], w=[("gsA", p, tp), ("ps", pb)])

                for tb in range(NTB):
                    items.append(lambda tb=tb: fm(0, tb))
                    items.append(lambda tb=tb: fm(1, tb))
                for tp in range(NTT // 2):
                    items.append(lambda tp=tp: tm(tp))
                items.append(lambda: wrelease(wpos))
                return items

            def qg_items():
                qgT_ = Arena().take([4, T], BF16)
                alias = ([("qT", 0, t) for t in range(NTB)] + [("kT", 0, t) for t in range(NTB)]
                         + [("V", 0, t) for t in range(NTT // 2)] + [("gsA", 0, t) for t in range(NTT // 2)]
                         + [("vaug_ones", 0)])
                wi, wpos_q = wnext(G_Q)
                W = wb[wi]
                kw = ("wb", wi)
                items = []

                def one(cb, tb, n_ev):
                    pb = proj_ring["n"] % 2
                    proj_ring["n"] += 1
                    for kc in range(KC):
                        S.add("pe", lambda e, kc=kc: e.matmul(
                            bank(pb), lhsT=W[:, kc, cb * 128:(cb + 1) * 128],
                            rhs=hT[:, kc, tb * 512:(tb + 1) * 512], start=(kc == 0), stop=(kc == KC - 1)),
                            r=[kw] + hT_all[tb * 4:tb * 4 + 4], w=[("ps", pb)])
                    dst = qgT_[:, cb, tb * 512:(tb + 1) * 512]
                    if n_ev % 2 == 0:
                        S.add("act", lambda e: e.activation(out=dst, in_=bank(pb), func=AF.Copy, scale=128 ** -0.5),
                              r=[("ps", pb)], w=[("qgT", tb), ("ps", pb)] + alias)
                    else:
                        S.add("dve", lambda e: e.tensor_scalar(
                            out=dst, in0=bank(pb), scalar1=128 ** -0.5, scalar2=None, op0=ALU.mult),
                            r=[("ps", pb)], w=[("qgT", tb), ("ps", pb)] + alias)

                n_ev = 0
                for cb in range(4):
                    for tb in range(NTB):
                        items.append(lambda cb=cb, tb=tb, n_ev=n_ev: one(cb, tb, n_ev))
                        n_ev += 1
                items.append(lambda: wrelease(wpos_q))
                return items

            pending = proj_items(0)
            for hp in range(8):
                p = hp % 2
                qT, kT, Vaug, gsA = qTb[p], kTb[p], Vaugb[p], gsAb[p]
                for f in pending:
                    f()
                if hp + 1 < 8:
                    pending = proj_items(hp + 1)
                elif "C" in phases:
                    pending = qg_items()
                else:
                    pending = []
                for h in range(2):
                    head = hp * 2 + h
                    S.add("sp", lambda e, h=h, head=head: e.dma_start(out=eB[:, h, :], in_=relb_d[head]),
                          w=[("eBl", h)], dma="eB%d" % h)
                    S.add("pool", lambda e, h=h: e.memset(eB[64:128, h, 0:64], NEG), r=[("eBl", h)], w=[("eBl", h)])
                    S.add("pool", lambda e, h=h: e.memset(eB[0:64, h, 576:640], NEG), r=[("eBl", h)], w=[("eBl", h)])
                S.add("act", lambda e: e.activation(out=eBb, in_=eB, func=AF.Exp),
                      r=[("eBl", 0), ("eBl", 1)], w=["eB"])

                def qk_step(kt, p=p, qT=qT, kT=kT):
                    nq = min(640, T - 128 * kt)
                    nmain = min(512, nq)
                    ntail = nq - nmain
                    sbi = kt % 2
                    sb = sbp[sbi]
                    rk = [("kT", p, kt // 4)] + [("qT", p, t) for t in range(kt // 4, min(NTB, (128 * kt + nq - 1) // 512 + 1))]
                    sms = []
                    for h in range(2):
                        sm = 2 + smain_ring["n"] % 3
                        smain_ring["n"] += 1
                        sms.append(sm)
                        lhsT = kT[64 * h:64 * h + 64, 128 * kt:128 * kt + 128]
                        S.add("pe", lambda e, sm=sm, lhsT=lhsT, h=h: e.matmul(
                            bank(sm, nmain), lhsT=lhsT, rhs=qT[64 * h:64 * h + 64, 128 * kt:128 * kt + nmain],
                            start=True, stop=True), r=rk, w=[("ps", sm)])
                    if ntail:
                        for h in range(2):
                            lhsT = kT[64 * h:64 * h + 64, 128 * kt:128 * kt + 128]
                            S.add("pe", lambda e, lhsT=lhsT, h=h: e.matmul(
                                bank(5 + h, ntail), lhsT=lhsT,
                                rhs=qT[64 * h:64 * h + 64, 128 * kt + 512:128 * kt + 512 + ntail],
                                start=True, stop=True), r=rk, w=[("ps", 5 + h)])
                        for h in range(2):
                            S.add("act", lambda e, h=h: e.activation(
                                out=sb[:, h, 512:512 + ntail], in_=bank(5 + h, ntail), func=AF.Exp),
                                r=[("ps", 5 + h)], w=[("sb", sbi, h), ("ps", 5 + h)])
                    for h in range(2):
                        S.add("act", lambda e, h=h, sm=sms[h]: e.activation(
                            out=sb[:, h, 0:nmain], in_=bank(sm, nmain), func=AF.Exp),
                            r=[("ps", sms[h])], w=[("sb", sbi, h), ("ps", sms[h])])
                    S.add("dve", lambda e: e.tensor_tensor(
                        out=PT[kt % 6][:, :, 0:nq], in0=sb[:, :, 0:nq], in1=eBb[:, :, 0:nq], op=ALU.mult),
                        r=[("sb", sbi, 0), ("sb", sbi, 1), "eB"], w=[("PT", kt % 6, 0), ("PT", kt % 6, 1)])

                def pv_step(qt, p=p, Vaug=Vaug, gsA=gsA):
                    pv = bank(7, 130).rearrange("p (h d) -> p h d", h=2)
                    kts = list(range(max(0, qt - 4), qt + 1))
                    for h in range(2):
                        for i, k2 in enumerate(kts):
                            S.add("pe", lambda e, h=h, k2=k2, i=i: e.matmul(
                                pv[:, h, :], lhsT=PT[k2 % 6][:, h, (qt - k2) * 128:(qt - k2 + 1) * 128],
                                rhs=Vaug[:, k2, h, :], start=(i == 0), stop=(i == len(kts) - 1)),
                                r=[("PT", k2 % 6, h), ("V", p, k2 // 2), ("vaug_ones", p)], w=[("ps", 7)])
                    c = stat_col()
                    rden = stat[:, c:c + 2]
                    S.add("dve", lambda e: e.reciprocal(out=rden, in_=pv[:, :, 64]),
                          r=[("ps", 7)], w=[("stat", c), ("ps", 7)])
                    for h in range(2):
                        S.add("dve", lambda e, h=h: e.scalar_tensor_tensor(
                            out=zall[:, qt, 64 * h:64 * h + 64], in0=pv[:, h, 0:64], scalar=rden[:, h:h + 1],
                            in1=gsA[:, qt, 64 * h:64 * h + 64], op0=ALU.mult, op1=ALU.mult),
                            r=[("ps", 7), ("stat", c), ("gsA", p, qt // 2)], w=[("zall", qt), ("ps", 7)])

                for step in range(NTT + 1):
                    if step < NTT:
                        qk_step(step)
                    if pending:
                        pending.pop(0)()
                    if step >= 1:
                        pv_step(step - 1)
                for qb in range(4):
                    pb = proj_ring["n"] % 2
                    proj_ring["n"] += 1
                    for j in range(4):
                        qt = qb * 4 + j
                        S.add("pe", lambda e, pb=pb, j=j, qt=qt: e.matmul(
                            bank(pb, 128, j * 128), lhsT=zall[:, qt, :], rhs=ident[:], start=True, stop=True),
                            r=[("zall", qt), "ident"], w=[("ps", pb)])
                    if qb % 2 == 0:
                        S.add("act", lambda e, pb=pb, qb=qb, hp=hp: e.copy(
                            out=zaT[:, hp, qb * 512:(qb + 1) * 512], in_=bank(pb)),
                            r=[("ps", pb)], w=[("zaT", qb // 2), ("ps", pb)])
                    else:
                        S.add("dve", lambda e, pb=pb, qb=qb, hp=hp: e.tensor_copy(
                            out=zaT[:, hp, qb * 512:(qb + 1) * 512], in_=bank(pb)),
                            r=[("ps", pb)], w=[("zaT", qb // 2), ("ps", pb)])

            for f in pending:
                f()
            pending = []

            if "zaT" in dbg_d and seq == 0:
                S.barrier()
                D2 = Arena()
                D2.off = 56 * 1024
                tmp = D2.take([2048], F32)
                for kc in range(KC):
                    S.add("dve", lambda e, kc=kc: e.tensor_copy(out=tmp, in_=zaT[:, kc, :]),
                          r=[("zaT", 0), ("zaT", 1)], w=["dbgtmp"])
                    S.add("sp", lambda e, kc=kc: e.dma_start(out=dbg_d["zaT"][:, kc * T:(kc + 1) * T], in_=tmp),
                          r=["dbgtmp"], dma="dbg")


        if "C" in phases:
            S.barrier()
            C = Arena()
            qgT = C.take([4, T], BF16)
            codeT = C.take([T], BF16)
            decT = C.take([NTT, 4, 2], F32)
            Sst = C.take([4, 256], F32)
            Sb = [C.take([4, 256], BF16) for _ in range(2)]
            gnb = C.take([1024], F32)
            lR = [C.take([2, 512], BF16) for _ in range(2)]
            l32 = [C.take([2, 512], F32) for _ in range(2)]
            erevR = [C.take([512], F32) for _ in range(2)]
            kdecR = [C.take([512], BF16) for _ in range(2)]
            vR = [C.take([1024], BF16) for _ in range(2)]
            gsR = [C.take([1024], F32) for _ in range(2)]
            zR = [C.take([1024], BF16) for _ in range(2)]
            rr = {"n": 0}

            def nb(k=1):
                if k == 2 and rr["n"] % 2:
                    rr["n"] += 1
                b_ = rr["n"] % 8
                rr["n"] += k
                return b_

            hT_all = [("hT", t) for t in range(NTT)]
            S.add("sp", lambda e: e.dma_start(out=gnb, in_=vecs_d[2:3, :].partition_broadcast(128)),
                  w=["gnb"], dma="c_gnb")
            S.add("dve", lambda e: e.tensor_scalar(out=gnb, in0=gnb, scalar1=0.5, scalar2=None, op0=ALU.mult),
                  r=["gnb"], w=["gnb"])
            S.add("pool", lambda e: e.memset(codeT[0:32, :], 1.0), w=["codeT"])
            S.add("pool", lambda e: e.memset(Sst, 0.0), w=[("Sst", h) for h in range(4)])
            if "B" not in phases:
                wi, wpos_q = wnext(G_Q)
                W = wb[wi]
                kw = ("wb", wi)
                n_ev = 0
                for cb in range(4):
                    for tb in range(NTB):
                        pb = nb()
                        for kc in range(KC):
                            S.add("pe", lambda e, pb=pb, kc=kc, tb=tb, cb=cb, W=W: e.matmul(
                                bank(pb), lhsT=W[:, kc, cb * 128:(cb + 1) * 128],
                                rhs=hT[:, kc, tb * 512:(tb + 1) * 512], start=(kc == 0), stop=(kc == KC - 1)),
                                r=[kw] + hT_all[tb * 4:tb * 4 + 4], w=[("ps", pb)])
                        dst = qgT[:, cb, tb * 512:(tb + 1) * 512]
                        if n_ev % 2 == 0:
                            S.add("act", lambda e, pb=pb, dst=dst: e.activation(
                                out=dst, in_=bank(pb), func=AF.Copy, scale=128 ** -0.5),
                                r=[("ps", pb)], w=[("qgT", tb), ("ps", pb)])
                        else:
                            S.add("dve", lambda e, pb=pb, dst=dst: e.tensor_scalar(
                                out=dst, in0=bank(pb), scalar1=128 ** -0.5, scalar2=None, op0=ALU.mult),
                                r=[("ps", pb)], w=[("qgT", tb), ("ps", pb)])
                        n_ev += 1
                wrelease(wpos_q)
            for tb in range(NTB):
                pb = nb()
                for kc in range(KC):
                    S.add("pe", lambda e, pb=pb, kc=kc, tb=tb: e.matmul(
                        ps[0:16, pb * 512:(pb + 1) * 512], lhsT=wcode[:, kc, :],
                        rhs=hT[:, kc, tb * 512:(tb + 1) * 512], start=(kc == 0), stop=(kc == KC - 1)),
                        r=["wcode"] + hT_all[tb * 4:tb * 4 + 4], w=[("ps", pb)])
                S.add("dve", lambda e, pb=pb, tb=tb: e.tensor_copy(
                    out=codeT[0:16, tb * 512:(tb + 1) * 512], in_=ps[0:16, pb * 512:(pb + 1) * 512]),
                    r=[("ps", pb), "codeT"], w=[("codeT", tb), ("ps", pb)])
            wk_i, wpos_k = wnext(G_K)
            wv_i = [wnext(G_V0)[0], wnext(G_V1)[0]]
            wgg_i = [wnext(G_G0)[0], wnext(G_G1)[0]]
            Wk = wb[wk_i]
            Wv = [wb[wv_i[0]], wb[wv_i[1]]]
            Wg = [wb[wgg_i[0]], wb[wgg_i[1]]]
            rr4 = {"n": 0}

            def nb4():
                b_ = rr4["n"] % 6
                rr4["n"] += 1
                return b_

            def tile_ctx(t):
                par = t % 2
                return dict(t=t, par=par, tb=t // 4, tok=slice(t * 128, (t + 1) * 128), erev=erevR[par],
                            kdec=kdecR[par], vt=vR[par], gs=gsR[par], zt=zR[par], po2=6,
                            lT=None)

            def st_gate2(t):
                pp = (t // 2) % 2
                l3, lT2 = l32[pp], lR[pp]
                for j in range(2):
                    tt_ = t + j
                    tok = slice(tt_ * 128, (tt_ + 1) * 128)
                    p1 = nb4()
                    S.add("pe", lambda e, p1=p1, tok=tok: e.matmul(
                        bank(p1), lhsT=codeT[0:17, tok], rhs=gkup_b[:, :], start=True, stop=True),
                        r=["codeT", ("codeT", tt_ // 4), "gkup_b"], w=[("ps", p1)])
                    S.add("act", lambda e, p1=p1, j=j: e.activation(out=l3[:, j, :], in_=bank(p1), func=AF.Exp, scale=-1.0),
                          r=[("ps", p1)], w=[("l32", pp), ("ps", p1)])
                S.add("act", lambda e: e.activation(out=lT2, in_=l3, func=AF.Ln, bias=1.0),
                      r=[("l32", pp)], w=[("l", pp)])

            def st_rev(c):
                t, par, erev = c["t"], c["par"], c["erev"]
                pp = (t // 2) % 2
                lT = lR[pp][:, t % 2, :]
                p2 = nb4()
                S.add("pe", lambda e: e.matmul(bank(p2), lhsT=revtri_b[:, :], rhs=lT, start=True, stop=True),
                      r=[("l", pp), "revtri_b"], w=[("ps", p2)])
                S.add("act", lambda e: e.activation(out=erev, in_=bank(p2), func=AF.Exp, scale=-1.0 / 16.0),
                      r=[("ps", p2)], w=[("erev", par), ("ps", p2)])
                p4 = nb4()
                for h in range(4):
                    S.add("pe", lambda e, h=h: e.matmul(
                        bank(p4, 2, h * 2), lhsT=lT[:, h * 128:(h + 1) * 128], rhs=cind_b[:, :], start=True, stop=True),
                        r=[("l", pp), "cind_b"], w=[("ps", p4)])
                S.add("act", lambda e: e.activation(
                    out=decT[:, t, :, :], in_=bank(p4, 8).rearrange("p (h c) -> p h c", h=4), func=AF.Exp, scale=-1.0 / 16.0),
                    r=[("ps", p4)], w=[("decT", t), ("ps", p4)])

            def st_kproj(c):
                t, par, tok, erev, kdec = c["t"], c["par"], c["tok"], c["erev"], c["kdec"]
                Wk_l = Wk
                p3 = nb4()
                for kc in range(KC):
                    S.add("pe", lambda e, kc=kc: e.matmul(
                        bank(p3), lhsT=hT[:, kc, tok], rhs=Wk_l[:, kc, :], start=(kc == 0), stop=(kc == KC - 1)),
                        r=[("wb", wk_i), ("hT", t)], w=[("ps", p3)])
                S.add("dve", lambda e: e.tensor_tensor(out=kdec, in0=bank(p3), in1=erev, op=ALU.mult),
                      r=[("ps", p3), ("erev", par)], w=[("kdec", par), ("ps", p3)])

            def st_vproj(c):
                t, par, tok, vt = c["t"], c["par"], c["tok"], c["vt"]
                Wv_l = list(Wv)
                for half in range(2):
                    pvb = nb4()
                    for kc in range(KC):
                        S.add("pe", lambda e, kc=kc, pvb=pvb, half=half: e.matmul(
                            bank(pvb), lhsT=hT[:, kc, tok], rhs=Wv_l[half][:, kc, :], start=(kc == 0), stop=(kc == KC - 1)),
                            r=[("wb", wv_i[half]), ("hT", t)], w=[("ps", pvb)])
                    S.add("dve", lambda e, half=half, pvb=pvb: e.tensor_copy(out=vt[:, half * 512:(half + 1) * 512], in_=bank(pvb)),
                          r=[("ps", pvb)], w=[("v", par, half), ("ps", pvb)])

            def st_gproj(c):
                t, par, tok, gs = c["t"], c["par"], c["tok"], c["gs"]
                Wg_l = list(Wg)
                for half in range(2):
                    pgb = nb4()
                    hs = slice(half * 512, (half + 1) * 512)
                    for kc in range(KC):
                        S.add("pe", lambda e, kc=kc, pgb=pgb, half=half: e.matmul(
                            bank(pgb), lhsT=hT[:, kc, tok], rhs=Wg_l[half][:, kc, :], start=(kc == 0), stop=(kc == KC - 1)),
                            r=[("wb", wgg_i[half]), ("hT", t)], w=[("ps", pgb)])
                    S.add("act", lambda e, hs=hs, pgb=pgb: e.activation(out=gs[:, hs], in_=bank(pgb), func=AF.Tanh, scale=0.5),
                          r=[("ps", pgb)], w=[("gs", par, half), ("ps", pgb)])
                    S.add("dve", lambda e, hs=hs, pgb=pgb: e.scalar_tensor_tensor(
                        out=gs[:, hs], in0=gs[:, hs], scalar=1.0, in1=bank(pgb), op0=ALU.add, op1=ALU.mult),
                        r=[("ps", pgb), ("gs", par, half)], w=[("gs", par, half), ("ps", pgb)])
                    S.add("pool", lambda e, hs=hs: e.tensor_tensor(out=gs[:, hs], in0=gs[:, hs], in1=gnb[:, hs], op=ALU.mult),
                          r=[("gs", par, half), "gnb"], w=[("gs", par, half)])

            def st_upd(c, ci):
                t, par, kdec, vt = c["t"], c["par"], c["kdec"], c["vt"]
                pus = [nb4(), nb4()]
                for h in range(4):
                    pbk = pus[h // 2]
                    S.add("pe", lambda e, h=h, pbk=pbk: e.matmul(
                        bank(pbk, 256, (h % 2) * 256), lhsT=kdec[64 * ci:64 * ci + 64, h * 128:(h + 1) * 128],
                        rhs=vt[64 * ci:64 * ci + 64, h * 256:(h + 1) * 256], start=True, stop=True),
                        r=[("kdec", par), ("v", par, h // 2)], w=[("ps", pbk)])
                for h in range(4):
                    pbk = pus[h // 2]
                    S.add("dve", lambda e, h=h, pbk=pbk: e.scalar_tensor_tensor(
                        out=Sst[:, h, :], in0=Sst[:, h, :], scalar=decT[:, t, h, ci:ci + 1],
                        in1=bank(pbk, 256, (h % 2) * 256), op0=ALU.mult, op1=ALU.add),
                        r=[("ps", pbk), ("decT", t), ("Sst", h)], w=[("Sst", h), ("ps", pbk)])
                S.add("act", lambda e: e.copy(out=Sb[ci], in_=Sst), r=[("Sst", h) for h in range(4)], w=[("Sb", ci)])

            def st_o(c, ci):
                t, tb, po2 = c["t"], c["tb"], c["po2"]
                ch = 2 * t + ci
                for h in range(4):
                    S.add("pe", lambda e, h=h: e.matmul(
                        ps[64 * ci:64 * ci + 64, (po2 + h // 2) * 512 + (h % 2) * 256:(po2 + h // 2) * 512 + (h % 2) * 256 + 256],
                        lhsT=qgT[:, h, ch * 64:(ch + 1) * 64], rhs=Sb[ci][:, h, :], start=True, stop=True),
                        r=[("Sb", ci), ("qgT", tb)], w=[("ps", po2 + h // 2)])

            def st_norm(c):
                t, par, po2, gs, zt = c["t"], c["par"], c["po2"], c["gs"], c["zt"]
                cst = stat_col()
                ssq = stat[:, cst:cst + 4]
                kst = ("stat", cst)
                osrc = ps[:, po2 * 512:(po2 + 2) * 512].rearrange("p (h d) -> p h d", h=4)
                for h in range(4):
                    pk = ("ps", po2 + h // 2)
                    S.add("act", lambda e, h=h: e.activation(
                        out=zt[:, h * 256:(h + 1) * 256], in_=osrc[:, h, :], func=AF.Square, accum_out=ssq[:, h:h + 1]),
                        r=[pk], w=[kst, ("z", par), pk])
                S.add("dve", lambda e: e.tensor_scalar(out=ssq, in0=ssq, scalar1=1.0 / 256.0, scalar2=EPS,
                                                       op0=ALU.mult, op1=ALU.add), r=[kst], w=[kst])
                S.add("pool", lambda e: e.tensor_tensor(out=ssq, in0=ssq, in1=neghalf[:, 0:4], op=ALU.pow),
                      r=[kst, "neghalf"], w=[kst])

                def zmul():
                    for h in range(4):
                        pk = ("ps", po2 + h // 2)
                        S.add("dve", lambda e, h=h: e.scalar_tensor_tensor(
                            out=zt[:, h * 256:(h + 1) * 256], in0=osrc[:, h, :], scalar=ssq[:, h:h + 1],
                            in1=gs[:, h * 256:(h + 1) * 256], op0=ALU.mult, op1=ALU.mult),
                            r=[pk, kst, ("gs", par, h // 2)], w=[("z", par), pk])
                return zmul

            def st_tr(c):
                t, par, tok, zt = c["t"], c["par"], c["tok"], c["zt"]
                for half in range(2):
                    ptr = nb4()
                    for j in range(4):
                        kc = half * 4 + j
                        S.add("pe", lambda e, kc=kc, j=j, ptr=ptr: e.matmul(
                            bank(ptr, 128, j * 128), lhsT=zt[:, kc * 128:(kc + 1) * 128], rhs=ident[:],
                            start=True, stop=True),
                            r=[("z", par), "ident"], w=[("ps", ptr)])
                    src = bank(ptr).rearrange("p (a b) -> p a b", a=4)
                    dst = zgT[:, half * 4:(half + 1) * 4, tok]
                    if half == 0:
                        S.add("act", lambda e, src=src, dst=dst: e.copy(out=dst, in_=src),
                              r=[("ps", ptr)], w=[("zgT", t), ("ps", ptr)])
                    else:
                        S.add("dve", lambda e, src=src, dst=dst: e.tensor_copy(out=dst, in_=src),
                              r=[("ps", ptr)], w=[("zgT", t), ("ps", ptr)])

            for it in range(NTT + 3):
                c0 = tile_ctx(it) if it < NTT else None
                c1 = tile_ctx(it - 1) if 0 <= it - 1 < NTT else None
                c2 = tile_ctx(it - 2) if 0 <= it - 2 < NTT else None
                c3 = tile_ctx(it - 3) if 0 <= it - 3 < NTT else None
                zmul = st_norm(c3) if c3 else None
                if c0 and it % 2 == 0:
                    st_gate2(it)
                if c2:
                    st_upd(c2, 0)
                if zmul:
                    zmul()
                if c2:
                    st_upd(c2, 1)
                if c1:
                    st_rev(c1)
                    st_kproj(c1)
                    st_vproj(c1)
                if c2:
                    st_o(c2, 0)
                    st_o(c2, 1)
                    st_gproj(c2)
                if c3:
                    st_tr(c3)

            for dpos in range(5):
                wrelease(wpos_k + dpos)

            if "zgT" in dbg_d and seq == 0:
                S.barrier()
                D2 = Arena()
                D2.off = 56 * 1024
                tmp = D2.take([2048], F32)
                for kc in range(KC):
                    S.add("dve", lambda e, kc=kc: e.tensor_copy(out=tmp, in_=zgT[:, kc, :]),
                          r=[("zgT", t) for t in range(NTT)], w=["dbgtmp"])
                    S.add("sp", lambda e, kc=kc: e.dma_start(out=dbg_d["zgT"][:, kc * T:(kc + 1) * T], in_=tmp),
                          r=["dbgtmp"], dma="dbg")


        if "D" in phases:
            S.barrier()
            Dn = Arena()
            mT = Dn.take([KC, T], BF16)
            tgR = [Dn.take([2, 512], F32) for _ in range(2)]
            t1R = [Dn.take([512], F32) for _ in range(1)]
            xtD = [Dn.take([1024], F32) for _ in range(2)]
            yoD = [Dn.take([1024], F32) for _ in range(2)]
            gpost = Dn.take([1024], F32)
            sqD = [Dn.take([1024], BF16) for _ in range(2)]
            rr = {"n": 0}

            def nb(k=1):
                if k == 2 and rr["n"] % 2:
                    rr["n"] += 1
                b_ = rr["n"] % 8
                rr["n"] += k
                return b_

            hT_all = [("hT", t) for t in range(NTT)]
            zg_all = [("zgT", t) for t in range(NTT)]
            S.add("sp", lambda e: e.dma_start(out=gpost, in_=vecs_d[1:2, :].partition_broadcast(128)),
                  w=["gpost"], dma="c_gpost")
            it = 0
            for dc in range(8):
                wi, wpos = wnext(G_D0 + dc)
                W = wb[wi]
                kw = ("wb", wi)
                for tb in range(NTB):
                    tsl = slice(tb * 512, (tb + 1) * 512)
                    tg = tgR[it % 2]
                    t1 = t1R[0]
                    ktg = ("tgD", it % 2)
                    it += 1
                    pbs = []
                    for j, (src, rkeys) in enumerate(((hT, hT_all[tb * 4:tb * 4 + 4]), (hT, hT_all[tb * 4:tb * 4 + 4]),
                                                     (zgT, zg_all[tb * 4:tb * 4 + 4]), (zaT, [("zaT", tb // 2)]))):
                        pb = nb()
                        pbs.append(pb)
                        for kc in range(KC):
                            S.add("pe", lambda e, pb=pb, kc=kc, j=j, src=src, W=W, tsl=tsl: e.matmul(
                                bank(pb), lhsT=W[:, kc, j * 128:(j + 1) * 128], rhs=src[:, kc, tsl],
                                start=(kc == 0), stop=(kc == KC - 1)),
                                r=[kw] + rkeys, w=[("ps", pb)])
                        if j < 2:
                            col = dc if j == 0 else 8 + dc
                            S.add("act", lambda e, pb=pb, j=j, tg=tg, col=col: e.activation(
                                out=tg[:, j, :], in_=bank(pb), func=AF.Tanh, bias=mbias[:, col:col + 1], scale=0.5),
                                r=[("ps", pb), "mbias"], w=[ktg, ("ps", pb)])
                    S.add("dve", lambda e, tg=tg, t1=t1, pb=pbs[2]: e.scalar_tensor_tensor(
                        out=t1, in0=tg[:, 0, :], scalar=1.0, in1=bank(pb), op0=ALU.add, op1=ALU.mult),
                        r=[ktg, ("ps", pbs[2])], w=["t1", ("ps", pbs[2])])
                    S.add("dve", lambda e, tg=tg, pb=pbs[3]: e.scalar_tensor_tensor(
                        out=tg[:, 1, :], in0=tg[:, 1, :], scalar=1.0, in1=bank(pb), op0=ALU.add, op1=ALU.mult),
                        r=[ktg, ("ps", pbs[3])], w=[ktg, ("ps", pbs[3])])
                    S.add("pool", lambda e, tg=tg, t1=t1, dc=dc, tsl=tsl: e.tensor_tensor(
                        out=mT[:, dc, tsl], in0=t1, in1=tg[:, 1, :], op=ALU.add),
                        r=["t1", ktg], w=[("mT", tb)])
                wrelease(wpos)
            wo0, wpos_o = wnext(G_O0)
            wo_i = [wo0, wnext(G_O0 + 1)[0]]
            Wo_l = [wb[wo_i[0]], wb[wo_i[1]]]

            def d_mm(tt, seq=seq):
                par = tt % 2
                tok = slice(tt * 128, (tt + 1) * 128)
                xi, sq = xtD[par], sqD[par]
                kx = ("xtD", par)
                S.add("sp", lambda e: e.dma_start(out=xi, in_=x_d[seq, tok, :]), w=[kx], dma="xtD%d" % par)
                py = nb(2)
                for half in range(2):
                    for kc in range(KC):
                        S.add("pe", lambda e, half=half, kc=kc: e.matmul(
                            bank(py + half), lhsT=mT[:, kc, tok], rhs=Wo_l[half][:, kc, :],
                            start=(kc == 0), stop=(kc == KC - 1)),
                            r=[("wb", wo_i[half]), ("mT", tt // 4)], w=[("ps", py + half)])
                ysrc = ps[:, py * 512:(py + 2) * 512]
                pyk = [("ps", py), ("ps", py + 1)]
                cst = stat_col()
                ssq = stat[:, cst:cst + 1]
                kst = ("stat", cst)
                S.add("act", lambda e: e.activation(out=sq, in_=ysrc, func=AF.Square, accum_out=ssq),
                      r=pyk, w=[kst, ("sqD", par)] + pyk)
                S.add("dve", lambda e: e.tensor_scalar(out=ssq, in0=ssq, scalar1=1.0 / D, scalar2=4.0 * EPS,
                                                       op0=ALU.mult, op1=ALU.add), r=[kst], w=[kst])
                S.add("pool", lambda e: e.tensor_tensor(out=ssq, in0=ssq, in1=neghalf[:, 0:1], op=ALU.pow),
                      r=[kst, "neghalf"], w=[kst])
                return ysrc, pyk, ssq, kst

            def d_fin(tt, ysrc, pyk, ssq, kst, seq=seq):
                par = tt % 2
                tok = slice(tt * 128, (tt + 1) * 128)
                xi, yo = xtD[par], yoD[par]
                kx = ("xtD", par)
                S.add("dve", lambda e: e.scalar_tensor_tensor(
                    out=yo, in0=ysrc, scalar=ssq, in1=gpost, op0=ALU.mult, op1=ALU.mult),
                    r=pyk + [kst, "gpost"], w=[("yoD", par)] + pyk)
                S.add("pool", lambda e: e.tensor_tensor(out=xi, in0=xi, in1=yo, op=ALU.add),
                      r=[kx, ("yoD", par)], w=[kx])
                S.add("sp", lambda e: e.dma_start(out=out_d[seq, tok, :], in_=xi), r=[kx], dma="oD%d" % par)

            d_st = {}
            for it in range(NTT + 1):
                if it < NTT:
                    d_st[it] = d_mm(it)
                if it >= 1:
                    d_fin(it - 1, *d_st[it - 1])
            wrelease(wpos_o)
            wrelease(wpos_o + 1)

    w_try_issue(limit=2)
    for seq_ in range(nseq):
        do_seq(seq_)

    S.finalize()
    sem_names = list(S.ENGS[:4]) + S.dma_sems
    import contextlib
    with contextlib.ExitStack() as es:
        sems = {n: es.enter_context(nc.semaphore("s_" + str(n))) for n in sem_names}
        with nc.Block() as block:
            @block.tensor
            def _(e):
                S.emit("pe", e, sems)

            @block.scalar
            def _(e):
                S.emit("act", e, sems)

            @block.vector
            def _(e):
                S.emit("dve", e, sems)

            @block.gpsimd
            def _(e):
                S.emit("pool", e, sems)

            @block.sync
            def _(e):
                S.emit("sp", e, sems)
                for n in S.dma_sems:
                    e.wait_ge(sems[n], S.final_counts[n])
    return nc


def _prep_shared(norm_pre_g, w_in, gk_up, gk_bias, gla_norm_g, rel_bias, w_o_gla, w_o_att,
                 merge_bias, w_out, norm_post_g):
    w_in = np.asarray(w_in, np.float32)
    o_qg, o_kg, o_vg, o_gg, o_code = 0, 512, 1024, 2048, 3072
    o_qa, o_ka, o_va, o_ga, o_gate = 3088, 4112, 5136, 6160, 7184
    groups = []
    for hp in range(8):
        cols = np.concatenate([np.arange(o + hp * 128, o + hp * 128 + 128) for o in (o_qa, o_ka, o_va, o_ga)])
        groups.append(w_in[:, cols])
    groups.append(w_in[:, o_qg:o_qg + 512])
    groups.append(w_in[:, o_kg:o_kg + 512])
    groups.append(w_in[:, o_vg:o_vg + 512])
    groups.append(w_in[:, o_vg + 512:o_vg + 1024])
    groups.append(w_in[:, o_gg:o_gg + 512])
    groups.append(w_in[:, o_gg + 512:o_gg + 1024])
    w_o_gla = np.asarray(w_o_gla, np.float32)
    w_o_att = np.asarray(w_o_att, np.float32)
    w_out = np.asarray(w_out, np.float32)
    for dc in range(8):
        sl = slice(dc * 128, dc * 128 + 128)
        groups.append(np.concatenate([w_in[:, o_gate + dc * 128:o_gate + dc * 128 + 128],
                                      w_in[:, o_gate + 1024 + dc * 128:o_gate + 1024 + dc * 128 + 128],
                                      w_o_gla[:, sl], w_o_att[:, sl]], axis=1))
    groups.append(w_out[:, 0:512])
    groups.append(w_out[:, 512:1024])
    wg = np.stack(groups, 0)
    wg = wg.reshape(NGROUPS, KC, 128, 512).transpose(0, 2, 1, 3).reshape(NGROUPS, 128, KC * 512)
    wg = np.ascontiguousarray(wg)
    wcode = w_in[:, o_code:o_code + 16].reshape(KC, 128, 16).transpose(1, 0, 2).reshape(128, KC * 16)
    gkup = np.concatenate([np.asarray(gk_up, np.float32), np.asarray(gk_bias, np.float32)[None, :]], 0)
    vecs = np.stack([np.asarray(norm_pre_g, np.float32), np.asarray(norm_post_g, np.float32),
                     np.tile(np.asarray(gla_norm_g, np.float32), 4)], 0)
    mb = np.asarray(merge_bias, np.float32).reshape(16, 128).T
    kk = np.arange(128)[:, None]
    qq = np.arange(640)[None, :]
    relb = np.asarray(rel_bias, np.float32)[:, np.clip(qq - kk, -256, 256) + 256]
    ident = np.eye(128, dtype=np.float32)
    tp = np.arange(128)
    revtri = ((tp[:, None] > tp[None, :]) & (tp[:, None] // 64 == tp[None, :] // 64)).astype(np.float32)
    cind = (tp[:, None] // 64 == np.arange(2)[None, :]).astype(np.float32)
    return {
        "wg": wg, "wcode": np.ascontiguousarray(wcode), "gkup": np.ascontiguousarray(gkup),
        "vecs": np.ascontiguousarray(vecs), "mbias": np.ascontiguousarray(mb),
        "relb": np.ascontiguousarray(relb), "ident": ident, "revtri": revtri, "cind": cind,
    }


_NC_CACHE = {}


def kernel(x, norm_pre_g, w_in, gk_up, gk_bias, gla_norm_g, rel_bias, w_o_gla, w_o_att,
           merge_bias, w_out, norm_post_g, _debug=None, _nseq=NSEQ, _phases="ABCD"):
    x = np.asarray(x, np.float32)
    shared = _prep_shared(norm_pre_g, w_in, gk_up, gk_bias, gla_norm_g, rel_bias, w_o_gla, w_o_att,
                          merge_bias, w_out, norm_post_g)
    nc = build_nc(debug=_debug, nseq=_nseq, phases=_phases)
    in_maps = []
    for c in range(N_CORES):
        m = dict(shared)
        m["x"] = np.ascontiguousarray(x[2 * c:2 * c + 2])
        in_maps.append(m)
    res = run_bass_kernel_spmd(nc, in_maps, core_ids=list(range(N_CORES)))
    out = np.concatenate([np.asarray(r["out"]) for r in res.results], axis=0).astype(np.float32)
    if _debug is not None:
        return out, res.results
    return out
```
